# Optimizing a Trainium2 kernel written in Bass

```python
import jax, jax.numpy as jnp
from jax import lax
import numpy as np

D_MODEL = 1024
BATCH = 8
SEQ = 8192
DEPTH = 1

GRID_W = 64
CTX_LEN = 256
CHUNK = 64
EPS = 1e-6
N_MOD = 6
HG_HEADS = 4
HG_DK = 128
HG_DV = 128
HG_KEY = HG_HEADS * HG_DK
HG_WIDTH = HG_HEADS * HG_DV
GLA_HEADS = 4
GLA_DK = 64
GLA_DV = 128
GLA_QK = GLA_HEADS * GLA_DK
GLA_V = GLA_HEADS * GLA_DV
GLA_GATE_RANK = 16
GLA_TAU = 16.0
FFN_HIDDEN = 2816
CONV_K = 3
IN_SIZES = (HG_KEY, HG_KEY, HG_KEY, HG_WIDTH, HG_WIDTH,
            GLA_QK, GLA_QK, GLA_V, GLA_V,
            GLA_GATE_RANK, GLA_GATE_RANK,
            D_MODEL, D_MODEL)
IN_WIDTH = sum(IN_SIZES)

kernel_name = 'hybrid_hgrn2_gla_convglu_dit_block'


def rmsnorm(x, g):
    xf = x.astype(jnp.float32)
    y = xf * lax.rsqrt(jnp.mean(xf * xf, axis=-1, keepdims=True) + EPS)
    return (y * g.astype(jnp.float32)).astype(x.dtype)


def modulate(h, shift, scale):
    return h * (1.0 + scale) + shift


def heads(a, n):
    b, t, _ = a.shape
    return a.reshape(b, t, n, -1).transpose(0, 2, 1, 3)


def merge_heads(a):
    b, n, t, d = a.shape
    return a.transpose(0, 2, 1, 3).reshape(b, t, n * d)


def flip_t(a):
    return a[:, :, ::-1]


def chunk_gated_scan(q, k, v, log_f, s0):
    bsz, h, t, _ = q.shape
    dv = v.shape[-1]
    n = t // CHUNK

    def blk(a):
        return a.astype(jnp.float32).reshape(bsz, h, n, CHUNK, a.shape[-1]).transpose(2, 0, 1, 3, 4)

    qc, kc, vc, gc = blk(q), blk(k), blk(v), blk(log_f)
    b = jnp.cumsum(gc, axis=3)
    b_last = b[:, :, :, -1:, :]
    q_dec = qc * jnp.exp(b)
    k_inv = kc * jnp.exp(-b)
    k_end = kc * jnp.exp(b_last - b)
    mask = jnp.tril(jnp.ones((CHUNK, CHUNK), dtype=bool))
    att = jnp.where(mask, jnp.einsum('nbhid,nbhjd->nbhij', q_dec, k_inv), 0.0)
    o_intra = jnp.einsum('nbhij,nbhjv->nbhiv', att, vc)

    def step(s, xs):
        q_n, k_n, v_n, dec_n = xs
        o_n = jnp.einsum('bhid,bhdv->bhiv', q_n, s)
        s = dec_n[:, :, 0, :, None] * s + jnp.einsum('bhjd,bhjv->bhdv', k_n, v_n)
        return s, o_n

    _, o_inter = lax.scan(step, s0.astype(jnp.float32), (q_dec, k_end, vc, jnp.exp(b_last)))
    o = (o_intra + o_inter).transpose(1, 2, 0, 3, 4).reshape(bsz, h, t, dv)
    return o.astype(v.dtype)


def final_state(k, v, log_f):
    b = jnp.cumsum(log_f.astype(jnp.float32), axis=2)
    w = jnp.exp(b[:, :, -1:, :] - b)
    return jnp.einsum('bhtd,bhtv->bhdv', k.astype(jnp.float32) * w, v.astype(jnp.float32))


def bidir_scan(q, k_fw, k_bw, v, g_fw, g_bw, s_fw, s_bw):
    o_fw = chunk_gated_scan(q, k_fw, v, g_fw, s_fw)
    o_bw = chunk_gated_scan(flip_t(q), flip_t(k_bw), flip_t(v), flip_t(g_bw), s_bw)
    return o_fw + flip_t(o_bw)


def mixer_inputs(h, w_in, lb, gk_w2, gk_b):
    z = h @ w_in
    idx = np.cumsum(IN_SIZES)[:-1].tolist()
    (hq, hf_fw, hf_bw, hi, hog, gq, gk, gv, gog, gr_fw, gr_bw, ga, gb) = jnp.split(z, idx, axis=-1)

    def hg_forget(zf, lb_d):
        f = lb_d + (1.0 - lb_d) * jax.nn.sigmoid(zf.astype(jnp.float32))
        return heads(jnp.log(f), HG_HEADS), heads((1.0 - f).astype(h.dtype), HG_HEADS)

    def gla_decay(r, w2, b2):
        return heads(jax.nn.log_sigmoid((r @ w2 + b2).astype(jnp.float32)) / GLA_TAU, GLA_HEADS)

    hg_g_fw, hg_k_fw = hg_forget(hf_fw, lb[0])
    hg_g_bw, hg_k_bw = hg_forget(hf_bw, lb[1])
    return dict(
        hg_q=heads(jax.nn.silu(hq) * (HG_DK ** -0.5), HG_HEADS),
        hg_k_fw=hg_k_fw, hg_k_bw=hg_k_bw, hg_g_fw=hg_g_fw, hg_g_bw=hg_g_bw,
        hg_v=heads(hi, HG_HEADS), hg_og=hog,
        gla_q=heads(gq * (GLA_DK ** -0.5), GLA_HEADS), gla_k=heads(gk, GLA_HEADS), gla_v=heads(gv, GLA_HEADS),
        gla_g_fw=gla_decay(gr_fw, gk_w2[0], gk_b[0]), gla_g_bw=gla_decay(gr_bw, gk_w2[1], gk_b[1]),
        gla_og=gog, gate_a=ga, gate_b=gb)


def context_states(p):
    return (final_state(p['hg_k_fw'], p['hg_v'], p['hg_g_fw']),
            final_state(flip_t(p['hg_k_bw']), flip_t(p['hg_v']), flip_t(p['hg_g_bw'])),
            final_state(p['gla_k'], p['gla_v'], p['gla_g_fw']),
            final_state(flip_t(p['gla_k']), flip_t(p['gla_v']), flip_t(p['gla_g_bw'])))


def mixer_outputs(p, states, hg_onorm_g, gla_onorm_g, w_a, w_b, w_out):
    s_hg_fw, s_hg_bw, s_gla_fw, s_gla_bw = states
    o_hg = bidir_scan(p['hg_q'], p['hg_k_fw'], p['hg_k_bw'], p['hg_v'], p['hg_g_fw'], p['hg_g_bw'], s_hg_fw, s_hg_bw)
    y_hg = merge_heads(rmsnorm(o_hg, hg_onorm_g)) * jax.nn.silu(p['hg_og'])
    o_gla = bidir_scan(p['gla_q'], p['gla_k'], p['gla_k'], p['gla_v'], p['gla_g_fw'], p['gla_g_bw'], s_gla_fw, s_gla_bw)
    y_gla = merge_heads(rmsnorm(o_gla, gla_onorm_g)) * jax.nn.silu(p['gla_og'])
    m = jax.nn.sigmoid(p['gate_a']) * (y_hg @ w_a) + jax.nn.sigmoid(p['gate_b']) * (y_gla @ w_b)
    return m @ w_out


def dwconv_grid(u, w, b):
    bsz, t, f = u.shape
    rows = t // GRID_W
    y = lax.conv_general_dilated(u.reshape(bsz, rows, GRID_W, f), w[:, :, None, :].astype(u.dtype), (1, 1), 'SAME',
                                 dimension_numbers=('NHWC', 'HWIO', 'NHWC'), feature_group_count=f)
    return y.reshape(bsz, t, f) + b


def dwconv_seq(u, w, b):
    f = u.shape[-1]
    y = lax.conv_general_dilated(u, w[CONV_K // 2][:, None, :].astype(u.dtype), (1,), 'SAME',
                                 dimension_numbers=('NWC', 'WIO', 'NWC'), feature_group_count=f)
    return y + b


def conv_glu(h, w_up, conv_w, conv_b, w_down, on_grid):
    u, v = jnp.split(h @ w_up, 2, axis=-1)
    u = dwconv_grid(u, conv_w, conv_b) if on_grid else dwconv_seq(u, conv_w, conv_b)
    return (jax.nn.gelu(u) * v) @ w_down


def setup_inputs(seed: int = 0) -> dict:
    key = jax.random.key(seed)
    ks = jax.random.split(key, 24)
    f32 = jnp.float32
    D, F = D_MODEL, FFN_HIDDEN

    def nrm(k, shape, scale):
        return jax.random.normal(k, shape, f32) * scale

    return {
        'x': nrm(ks[0], (BATCH, SEQ, D), 1.0),
        'c': nrm(ks[1], (BATCH, D), 1.0),
        'ctx': nrm(ks[2], (BATCH, CTX_LEN, D), 1.0),
        'c_ctx': nrm(ks[3], (D,), 1.0),
        'ada_w': nrm(ks[4], (DEPTH, D, N_MOD * D), D ** -0.5),
        'ada_b': nrm(ks[5], (DEPTH, N_MOD * D), 0.02),
        'mix_pre_g': 1.0 + nrm(ks[6], (DEPTH, D), 0.02),
        'mix_post_g': 1.0 + nrm(ks[7], (DEPTH, D), 0.02),
        'ffn_pre_g': 1.0 + nrm(ks[8], (DEPTH, D), 0.02),
        'ffn_post_g': 1.0 + nrm(ks[9], (DEPTH, D), 0.02),
        'w_in': nrm(ks[10], (DEPTH, D, IN_WIDTH), D ** -0.5),
        'hg_lb_logits': nrm(ks[11], (2, DEPTH + 1, HG_KEY), 0.1),
        'hg_onorm_g': 1.0 + nrm(ks[12], (DEPTH, HG_DV), 0.02),
        'gla_gk_w2': nrm(ks[13], (DEPTH, 2, GLA_GATE_RANK, GLA_QK), GLA_GATE_RANK ** -0.5),
        'gla_gk_b': nrm(ks[14], (DEPTH, 2, GLA_QK), 0.1),
        'gla_onorm_g': 1.0 + nrm(ks[15], (DEPTH, GLA_DV), 0.02),
        'w_branch_a': nrm(ks[16], (DEPTH, HG_WIDTH, D), HG_WIDTH ** -0.5),
        'w_branch_b': nrm(ks[17], (DEPTH, GLA_V, D), GLA_V ** -0.5),
        'w_out': nrm(ks[18], (DEPTH, D, D), D ** -0.5),
        'ffn_w_up': nrm(ks[19], (DEPTH, D, 2 * F), D ** -0.5),
        'ffn_conv_w': nrm(ks[20], (DEPTH, CONV_K, CONV_K, F), 1.0 / CONV_K),
        'ffn_conv_b': nrm(ks[21], (DEPTH, F), 0.02),
        'ffn_w_down': nrm(ks[22], (DEPTH, F, D), F ** -0.5),
    }


def reference(x, c, ctx, c_ctx, ada_w, ada_b, mix_pre_g, mix_post_g, ffn_pre_g, ffn_post_g, w_in, hg_lb_logits,
              hg_onorm_g, gla_gk_w2, gla_gk_b, gla_onorm_g, w_branch_a, w_branch_b, w_out, ffn_w_up, ffn_conv_w,
              ffn_conv_b, ffn_w_down):
    lb_all = jnp.cumsum(jax.nn.softmax(hg_lb_logits.astype(jnp.float32), axis=1), axis=1)
    for layer in range(DEPTH):
        last = layer == DEPTH - 1
        mod_lat = (jax.nn.silu(c) @ ada_w[layer] + ada_b[layer])[:, None, :]
        mod_ctx = jax.nn.silu(c_ctx) @ ada_w[layer] + ada_b[layer]
        sh1, sc1, gt1, sh2, sc2, gt2 = jnp.split(mod_lat, N_MOD, axis=-1)
        csh1, csc1, cgt1, csh2, csc2, cgt2 = jnp.split(mod_ctx, N_MOD, axis=-1)
        in_w = (w_in[layer], lb_all[:, layer], gla_gk_w2[layer], gla_gk_b[layer])
        out_w = (hg_onorm_g[layer], gla_onorm_g[layer], w_branch_a[layer], w_branch_b[layer], w_out[layer])

        p_ctx = mixer_inputs(modulate(rmsnorm(ctx, mix_pre_g[layer]), csh1, csc1), *in_w)
        states = context_states(p_ctx)
        p_lat = mixer_inputs(modulate(rmsnorm(x, mix_pre_g[layer]), sh1, sc1), *in_w)
        y = mixer_outputs(p_lat, states, *out_w)
        x = x + gt1 * rmsnorm(y, mix_post_g[layer])
        if not last:
            zero_states = tuple(jnp.zeros_like(s) for s in states)
            y_ctx = mixer_outputs(p_ctx, zero_states, *out_w)
            ctx = ctx + cgt1 * rmsnorm(y_ctx, mix_post_g[layer])

        h = modulate(rmsnorm(x, ffn_pre_g[layer]), sh2, sc2)
        y = conv_glu(h, ffn_w_up[layer], ffn_conv_w[layer], ffn_conv_b[layer], ffn_w_down[layer], True)
        x = x + gt2 * rmsnorm(y, ffn_post_g[layer])
        if not last:
            hc = modulate(rmsnorm(ctx, ffn_pre_g[layer]), csh2, csc2)
            yc = conv_glu(hc, ffn_w_up[layer], ffn_conv_w[layer], ffn_conv_b[layer], ffn_w_down[layer], False)
            ctx = ctx + cgt2 * rmsnorm(yc, ffn_post_g[layer])
    return x
```

```python
import contextlib
import numpy as np
import concourse.bass as bass
import concourse.mybir as mybir
from concourse.bass_utils import run_bass_kernel_spmd

F32 = mybir.dt.float32
BF16 = mybir.dt.bfloat16
AF = mybir.ActivationFunctionType
ALU = mybir.AluOpType
AX = mybir.AxisListType

ENGS = ("tensor", "vector", "scalar", "gpsimd", "sync")
EPOCH = 20000
import os as _os
NOSYNC_ENGS = set(x for x in _os.environ.get('NOSYNC', '').split(',') if x)

D = 1024
CTX = 256
TT = 128
EPS = 1e-6
FH = 2816
NFB = 22
NB = 53200


class Buf:
    __slots__ = ("name", "last_w", "readers", "dsem", "dcount")

    def __init__(self, name):
        self.name = name
        self.last_w = None
        self.readers = []
        self.dsem = None
        self.dcount = 0


class Op:
    __slots__ = ("eng", "fn", "deps", "signal", "tok", "is_dma", "buf", "qi")

    def __init__(self, eng, fn):
        self.eng = eng
        self.fn = fn
        self.deps = []
        self.signal = False
        self.tok = None
        self.is_dma = False
        self.buf = None


class Prog:
    def __init__(self, nc):
        self.nc = nc
        self.q = {e: [] for e in ENGS}
        self.dma_bufs = []
        self.pe_skip = True
        self.last_pe = None
        self.last_pe_mode = (128, 128)

    def op(self, eng, fn, reads=(), writes=(), dma=None, extra=(), mode=(128, 128)):
        o = Op(eng, fn)
        deps = list(extra)
        force = None
        if eng == "tensor" and fn is not None:
            if mode != self.last_pe_mode and self.last_pe is not None:
                force = self.last_pe
            self.last_pe = o
            self.last_pe_mode = mode
        for b in reads:
            if b.last_w is not None:
                deps.append(b.last_w)
        for b in writes:
            if b.last_w is not None:
                deps.append(b.last_w)
            deps.extend(b.readers)
        if force is not None:
            deps.append(force)
        seen = set()
        for d in deps:
            if self.pe_skip and eng == "tensor" and d.eng == "tensor" and not d.is_dma and d is not force:
                continue
            if id(d) not in seen:
                seen.add(id(d))
                o.deps.append(d)
                d.signal = True
        for b in reads:
            b.readers.append(o)
        for b in writes:
            b.last_w = o
            b.readers = []
        if dma is not None:
            o.is_dma = True
            o.buf = dma
            if dma not in self.dma_bufs:
                self.dma_bufs.append(dma)
        self.q[eng].append(o)
        return o

    def barrier(self):
        lasts = []
        for e in ENGS:
            for o in reversed(self.q[e]):
                if not o.is_dma and o.fn is not None:
                    lasts.append(o)
                    break
        for b in self.dma_bufs:
            for e in ENGS:
                pass
        dl = {}
        for e in ENGS:
            for o in self.q[e]:
                if o.is_dma:
                    dl[id(o.buf)] = o
        lasts.extend(dl.values())
        for e in ENGS:
            self.op(e, None, extra=[l for l in lasts])

    def emit(self, final_waits=()):
        nc = self.nc
        with contextlib.ExitStack() as st:
            esems = {}
            for e in ENGS:
                n = sum(1 for o in self.q[e] if o.signal and not o.is_dma)
                esems[e] = [st.enter_context(nc.semaphore(f"s_{e}_{i}")) for i in range(n // EPOCH + 1)]
            for b in self.dma_bufs:
                b.dsem = st.enter_context(nc.semaphore(f"d_{b.name}"))
                b.dcount = 0
            for e in ENGS:
                c = 0
                for o in self.q[e]:
                    if o.is_dma:
                        o.buf.dcount += 16
                        o.tok = (o.buf.dsem, o.buf.dcount)
                        o.signal = True
                    elif o.signal:
                        if o.fn is None:
                            o.signal = False
                            continue
                        o.tok = (esems[e][c // EPOCH], c % EPOCH + 1)
                        c += 1
            block = st.enter_context(nc.Block())

            for e in ENGS:
                for qi, o in enumerate(self.q[e]):
                    o.qi = qi
            KD = int(_os.environ.get('KDIST', '3'))

            def run(eng_name):
                def body(eng):
                    waited = {}
                    for o in self.q[eng_name]:
                        for d in o.deps:
                            if d.tok is None:
                                continue
                            if eng_name in NOSYNC_ENGS and d.eng == eng_name and not d.is_dma:
                                continue
                            if KD > 0 and eng_name in ('vector', 'scalar') and d.eng == eng_name and not d.is_dma and o.qi - d.qi >= KD:
                                continue
                            s, v = d.tok
                            k = id(s)
                            if waited.get(k, 0) >= v:
                                continue
                            waited[k] = v
                            eng.wait_ge(s, v)
                        if o.fn is None:
                            continue
                        ins = o.fn(eng)
                        if o.signal:
                            s, v = o.tok
                            ins.then_inc(s, 16 if o.is_dma else 1)
                    if eng_name == "sync":
                        for o in final_waits:
                            s, v = o.tok
                            eng.wait_ge(s, v)
                return body

            block.tensor(run("tensor"))
            block.vector(run("vector"))
            block.scalar(run("scalar"))
            block.gpsimd(run("gpsimd"))
            block.sync(run("sync"))


C_HQ, C_HF, C_HI, C_HOG = 0, 512, 1536, 2048
C_GQ, C_GK, C_GV, C_GOG = 2560, 2816, 3072, 3584
C_GR, C_GA, C_GB = 4096, 4128, 5152
IN_W = 6176

FM_C, FM_CC, FM_ADAB, FM_PRE1, FM_PRE2, FM_LB, FM_GKB, FM_ON, FM_CW, FM_CB = 0, 8, 16, 48, 56, 64, 80, 84, 86, 284
NFM = 306


def build_nc(T=8192, debug=False):
    NT = T // TT
    ntiles = NT
    nffn = T // 512
    nc = bass.Bass("TRN2", target_bir_lowering=False)

    def din(name, shape):
        return nc.dram_tensor(name, shape, F32, kind="ExternalInput").ap()

    x = din("x", [T, D])
    ctx = din("ctx", [CTX, D])
    fmp_d = din("fmp", [128, NFM])
    rowp_d = din("rowp", [128, 4096])
    cst_d = din("cst", [128, 3, 128])
    w2_d = din("w2", [16, 2, 256])
    ada_w = din("ada_w", [D, 6 * D])
    w_in = din("w_in", [D, IN_W])
    w_a = din("w_a", [512, D])
    w_b = din("w_b", [512, D])
    w_o = din("w_o", [D, D])
    w_up = din("w_up", [D, 2 * FH])
    w_dn = din("w_dn", [FH, D])
    out = nc.dram_tensor("out", [T, D], F32, kind="ExternalOutput").ap()
    skind = "ExternalOutput" if debug else "Internal"
    obw = nc.dram_tensor("obw", [T, D], F32, kind=skind).ap()
    osum = nc.dram_tensor("osum", [T, D], F32, kind=skind).ap()
    x1s = nc.dram_tensor("x1s", [T, D], F32, kind=skind).ap()
    sq_q = nc.dram_tensor("sq_q", [T, 512], F32, kind="Internal").ap()
    sq_g = nc.dram_tensor("sq_g", [T, 512], F32, kind="Internal").ap()
    sq_v = nc.dram_tensor("sq_v", [T, 1024], BF16, kind="Internal").ap()

    import os
    _ps = os.environ.get('PESKIP', 'setup,scan,out,ffn').split(',')
    P = Prog(nc)
    P.pe_skip = 'setup' in _ps
    st = contextlib.ExitStack()
    big = st.enter_context(nc.sbuf_tensor("big", [128, NB], F32))
    PS = [st.enter_context(nc.psum_tensor(f"ps{i}", [128, 512], F32)) for i in range(8)]
    pb = [Buf(f"ps{i}") for i in range(8)]
    PSb = [p[:].bitcast(BF16) for p in PS]

    ptr = [0]
    nbuf = [0]

    def alloc(shape, dt=F32, parts=128):
        n = int(np.prod(shape))
        words = n if dt == F32 else (n + 1) // 2
        off = ptr[0]
        ptr[0] += words
        assert ptr[0] <= NB, ("SBUF overflow", ptr[0])
        ap = big[0:parts, off:off + words]
        if dt != F32:
            ap = ap.bitcast(dt)[:, 0:n]
        if len(shape) == 2:
            ap = ap.rearrange("p (a b) -> p a b", a=shape[0])
        elif len(shape) == 3:
            ap = ap.rearrange("p (a b c) -> p a b c", a=shape[0], b=shape[1])
        nbuf[0] += 1
        return ap, Buf(f"b{nbuf[0]}")

    dumps = {}

    def dump(name, ap, buf, parts=128):
        if not debug:
            return
        shp = [parts] + list(ap.shape[1:])
        dt_ = nc.dram_tensor("dbg_" + name, shp, F32, kind="ExternalOutput").ap()
        P.op("gpsimd", lambda e: e.dma_start(out=dt_, in_=ap), [buf], [], dma=Buf("dbg_" + name))

    def flat(ap):
        return ap.rearrange("p a b -> p (a b)")

    def V(fn, r=(), w=()):
        return P.op("vector", fn, r, w)

    def A(fn, r=(), w=()):
        return P.op("scalar", fn, r, w)

    def G(fn, r=(), w=()):
        return P.op("gpsimd", fn, r, w)

    def M(fn, r=(), w=(), mode=(128, 128)):
        return P.op("tensor", fn, r, w, mode=mode)

    def ld(eng, o_ap, i_ap, buf, r=()):
        return P.op(eng, lambda e: e.dma_start(out=o_ap, in_=i_ap), r, [buf], dma=buf)

    fmp, b_fmp = alloc([NFM])
    cst, b_cst = alloc([3, 128])
    identb, b_identb = alloc([128], BF16)
    w2sb, b_w2 = alloc([2, 256], BF16, parts=16)
    mhalf, b_mhalf = alloc([8])
    ones, b_ones = alloc([128])
    coef, b_coef = alloc([3, 8])
    nb2, b_nb2 = alloc([4])
    GB, b_GB = alloc([6, 8])
    Gt, b_Gt = alloc([2, D])
    ssx, b_ssx = alloc([16])
    zeros, b_zeros = alloc([64])
    persist_end = ptr[0]
    rowp, b_rowp = alloc([4096])

    ld("sync", fmp, fmp_d, b_fmp)
    ld("sync", rowp, rowp_d, b_rowp)
    ld("sync", cst, cst_d, b_cst)
    ld("gpsimd", w2sb, w2_d, b_w2)
    V(lambda e: e.tensor_copy(out=identb, in_=cst[:, 0, :]), [b_cst], [b_identb])
    G(lambda e: e.memset(mhalf, -0.5), [], [b_mhalf])
    G(lambda e: e.memset(ones, 1.0), [], [b_ones])
    G(lambda e: e.memset(zeros, 0.0), [], [b_zeros])
    maskf = cst[:, 1, :]
    maskb = cst[:, 2, :]

    tl, b_tl = alloc([8])
    lg = fmp[:, FM_LB:FM_LB + 16].rearrange("p (d s h) -> p d s h", d=2, s=2)
    V(lambda e: e.tensor_tensor(out=tl.rearrange("p (d h) -> p d h", d=2), in0=lg[:, :, 0, :], in1=lg[:, :, 1, :], op=ALU.subtract), [b_fmp], [b_tl])
    A(lambda e: e.activation(out=tl, in_=tl, func=AF.Tanh, scale=0.5), [b_tl], [b_tl])
    V(lambda e: e.tensor_scalar(out=coef[:, 0, :], in0=tl, scalar1=0.25, scalar2=0.75, op0=ALU.mult, op1=ALU.add), [b_tl], [b_coef])
    V(lambda e: e.tensor_scalar(out=coef[:, 1, :], in0=tl, scalar1=-0.25, scalar2=0.25, op0=ALU.mult, op1=ALU.add), [b_tl], [b_coef])
    V(lambda e: e.tensor_scalar(out=coef[:, 2, :], in0=tl, scalar1=0.25, scalar2=-0.25, op0=ALU.mult, op1=ALU.add), [b_tl], [b_coef])
    V(lambda e: e.tensor_scalar(out=nb2, in0=fmp[:, FM_GKB:FM_GKB + 4], scalar1=-1.0, scalar2=None, op0=ALU.mult), [b_fmp], [b_nb2])

    sc2, b_sc2 = alloc([8, 2])
    screp, b_screp = alloc([8, 128])
    tc_, b_tc = alloc([16])
    A(lambda e: e.activation(out=tc_, in_=fmp[:, 0:16], func=AF.Tanh, scale=0.5), [b_fmp], [b_tc])
    V(lambda e: e.scalar_tensor_tensor(out=tc_, in0=tc_, scalar=1.0, in1=fmp[:, 0:16], op0=ALU.add, op1=ALU.mult), [b_tc, b_fmp], [b_tc])
    V(lambda e: e.tensor_scalar(out=sc2.rearrange("p k j -> p j k"), in0=tc_.rearrange("p (j k) -> p j k", j=2), scalar1=0.5, scalar2=None, op0=ALU.mult), [b_tc], [b_sc2])
    for k in range(8):
        V(lambda e, k=k: e.tensor_scalar(out=screp[:, k, :], in0=ones, scalar1=sc2[:, k, 0:1], scalar2=None, op0=ALU.mult), [b_sc2, b_ones], [b_screp])
    stg = [alloc([8, D]) for _ in range(2)]
    modfm, b_modfm = alloc([4, 8, 2])
    aw = ada_w.rearrange("(k p) c -> p k c", p=128)
    fm_groups = {0: 0, 1: 1, 3: 2, 4: 3}
    for gi, g in enumerate([0, 1, 3, 4, 2, 5]):
        sg, bsg = stg[gi % 2]
        ld("sync", sg, aw[:, :, g * D:(g + 1) * D], bsg)
        if g in fm_groups:
            j = fm_groups[g]
            for cb in range(8):
                for k in range(8):
                    M(lambda e, sg=sg, cb=cb, k=k: e.matmul(PS[0][:, cb * 2:cb * 2 + 2], lhsT=sg[:, k, cb * 128:(cb + 1) * 128], rhs=sc2[:, k, :],
                                                              start=(cb == 0 and k == 0), stop=(k == 7), skip_group_check=True), [bsg, b_sc2], [pb[0]])
            V(lambda e, j=j: e.tensor_tensor(out=modfm[:, j, :, :], in0=PS[0][:, 0:16].rearrange("p (c j) -> p c j", j=2),
                                              in1=fmp[:, FM_ADAB + 8 * j:FM_ADAB + 8 * j + 8].unsqueeze(2).to_broadcast([128, 8, 2]), op=ALU.add), [pb[0], b_fmp], [b_modfm])
        else:
            gj = 0 if g == 2 else 1
            for hf in range(2):
                for k in range(8):
                    M(lambda e, sg=sg, hf=hf, k=k: e.matmul(PS[1 + hf][:, :], lhsT=screp[:, k, :], rhs=sg[:, k, hf * 512:(hf + 1) * 512],
                                                              start=(k == 0), stop=(k == 7)), [bsg, b_screp], [pb[1 + hf]])
                V(lambda e, gj=gj, hf=hf: e.tensor_tensor(out=Gt[:, gj, hf * 512:(hf + 1) * 512], in0=PS[1 + hf][:, :], in1=rowp[:, gj * D + hf * 512: gj * D + (hf + 1) * 512], op=ALU.add), [pb[1 + hf], b_rowp], [b_Gt])
            V(lambda e, gj=gj: e.tensor_tensor(out=Gt[:, gj, :], in0=Gt[:, gj, :], in1=rowp[:, (2 + gj) * D:(3 + gj) * D], op=ALU.mult), [b_rowp, b_Gt], [b_Gt])
    pre1 = fmp[:, FM_PRE1:FM_PRE1 + 8]
    pre2 = fmp[:, FM_PRE2:FM_PRE2 + 8]
    V(lambda e: e.scalar_tensor_tensor(out=GB[:, 0, :], in0=modfm[:, 1, :, 0], scalar=1.0, in1=pre1, op0=ALU.add, op1=ALU.mult), [b_modfm, b_fmp], [b_GB])
    V(lambda e: e.tensor_copy(out=GB[:, 1, :], in_=modfm[:, 0, :, 0]), [b_modfm], [b_GB])
    V(lambda e: e.scalar_tensor_tensor(out=GB[:, 2, :], in0=modfm[:, 1, :, 1], scalar=1.0, in1=pre1, op0=ALU.add, op1=ALU.mult), [b_modfm, b_fmp], [b_GB])
    V(lambda e: e.tensor_copy(out=GB[:, 3, :], in_=modfm[:, 0, :, 1]), [b_modfm], [b_GB])
    V(lambda e: e.scalar_tensor_tensor(out=GB[:, 4, :], in0=modfm[:, 3, :, 0], scalar=1.0, in1=pre2, op0=ALU.add, op1=ALU.mult), [b_modfm, b_fmp], [b_GB])
    V(lambda e: e.tensor_copy(out=GB[:, 5, :], in_=modfm[:, 2, :, 0]), [b_modfm], [b_GB])
    P.barrier()
    ptr[0] = persist_end

    def front(src_rows, xt, b_xt, junk, b_junk, xn, b_xn, hT, b_hT, ss, rstd, b_ss, gi, trbank, eps=EPS, xn_act=False):
        ld("sync", xt, src_rows, b_xt)
        A(lambda e: e.activation(out=junk, in_=xt, func=AF.Square, accum_out=ss), [b_xt], [b_junk, b_ss])
        G(lambda e: e.tensor_scalar(out=ss, in0=ss, scalar1=1.0 / D, scalar2=eps, op0=ALU.mult, op1=ALU.add), [b_ss], [b_ss])
        G(lambda e: e.tensor_tensor(out=rstd, in0=ss, in1=mhalf[:, 0:1], op=ALU.pow), [b_ss, b_mhalf], [b_ss])
        if xn_act:
            A(lambda e: e.activation(out=xn, in_=xt, func=AF.Copy, scale=rstd), [b_xt, b_ss], [b_xn])
        else:
            V(lambda e: e.tensor_scalar(out=xn, in0=xt, scalar1=rstd, scalar2=None, op0=ALU.mult), [b_xt, b_ss], [b_xn])
        for k in range(8):
            M(lambda e, k=k: e.transpose(PSb[trbank][:, k * 128:(k + 1) * 128], xn[:, k * 128:(k + 1) * 128], identb), [b_xn, b_identb], [pb[trbank]])
        for k in range(8):
            if k % 2 == 0:
                A(lambda e, k=k: e.activation(out=hT[:, k, :], in_=PSb[trbank][:, k * 128:(k + 1) * 128], func=AF.Identity, scale=GB[:, gi, k:k + 1], bias=GB[:, gi + 1, k:k + 1]), [pb[trbank], b_GB], [b_hT])
            else:
                V(lambda e, k=k: e.tensor_scalar(out=hT[:, k, :], in0=PSb[trbank][:, k * 128:(k + 1) * 128], scalar1=GB[:, gi, k:k + 1], scalar2=GB[:, gi + 1, k:k + 1], op0=ALU.mult, op1=ALU.add), [pb[trbank], b_GB], [b_hT])

    def fm_job(bank, w, b_w, col0, nblk, hT, b_hT, m=128):
        for cb in range(nblk):
            for k in range(8):
                M(lambda e, cb=cb, k=k: e.matmul(PS[bank][0:m, cb * 128:(cb + 1) * 128], lhsT=w[:, k, col0 + cb * m: col0 + (cb + 1) * m], rhs=hT[:, k, :],
                                                 start=(cb == 0 and k == 0), stop=(k == 7), skip_group_check=True), [b_w, b_hT], [pb[bank]])

    def tm_job(bank, w, b_w, col0, hT, b_hT):
        for k in range(8):
            M(lambda e, k=k: e.matmul(PS[bank][:, :], lhsT=hT[:, k, :], rhs=w[:, k, col0:col0 + 512], start=(k == 0), stop=(k == 7)), [b_w, b_hT], [pb[bank]])

    def run_round(items):
        items = [[g_, lim, 0] for (g_, lim) in items]
        while items:
            for it_ in list(items):
                if it_[1] is not None and it_[2] >= it_[1]:
                    items.remove(it_)
                    continue
                try:
                    next(it_[0])
                    it_[2] += 1
                except StopIteration:
                    items.remove(it_)

    def scan_pass(d, nt):
        ptr[0] = persist_end
        P.pe_skip = True
        wsc, _ = alloc([8, 2576], BF16)
        wi = w_in.rearrange("(k p) c -> p k c", p=128)
        bw = []
        for (dst, src, n) in [(0, C_HQ, 512), (512, C_HF + 512 * d, 512), (1024, C_HI, 512), (1536, C_GQ, 1024), (2560, C_GR + 16 * d, 16)]:
            b_ = Buf(f"w{d}_{dst}")
            bw.append(b_)
            P.op("gpsimd", lambda e, dst=dst, src=src, n=n: e.dma_start(out=wsc[:, :, dst:dst + n], in_=wi[:, :, src:src + n]), [], [b_], dma=b_)
        b_whq, b_whf, b_whi, b_wg, b_wgr = bw
        xts = [alloc([D]) for _ in range(2)]
        hTs = [alloc([8, 128], BF16) for _ in range(2)]
        sss = [alloc([2]) for _ in range(2)]
        tmps = []
        for _ in range(2):
            t_ = {}
            t_['junk'] = alloc([D], BF16); t_['xn'] = alloc([D], BF16)
            t_['tq'] = alloc([4, 128]); t_['qq'] = alloc([4, 128])
            t_['kk'] = alloc([4, 128]); t_['ff'] = alloc([4, 128]); t_['Ei'] = alloc([4, 128]); t_['cmh'] = alloc([4, 128])
            t_['kinv'] = alloc([4, 128], BF16); t_['kend'] = alloc([4, 128], BF16)
            t_['qg'] = alloc([2, 128]); t_['kg'] = alloc([2, 128])
            t_['eg'] = alloc([2, 128]); t_['cum'] = alloc([2, 128]); t_['Egi'] = alloc([2, 128])
            t_['kgi'] = alloc([2, 128], BF16); t_['kge'] = alloc([2, 128], BF16)
            t_['rT'] = alloc([128], BF16, parts=16)
            tmps.append(t_)
        EEs = [alloc([4, 128]) for _ in range(3)]
        Egs = [alloc([2, 128]) for _ in range(3)]
        qdecs = [alloc([4, 128], BF16) for _ in range(3)]
        qgds = [alloc([2, 128], BF16) for _ in range(3)]
        vhgs = [alloc([512], BF16) for _ in range(3)]
        vgls = [alloc([512], BF16) for _ in range(3)]
        ktms = [alloc([2, 768], BF16) for _ in range(3)]
        qdcs = [[alloc([4, 128], BF16) for _ in range(2)] for _ in range(3)]
        qgcs = [[alloc([2, 128], BF16) for _ in range(2)] for _ in range(3)]
        for sl_ in range(3):
            for c_ in range(2):
                G(lambda e, a_=qdcs[sl_][c_][0]: e.memset(a_, 0.0), [], [qdcs[sl_][c_][1]])
                G(lambda e, a_=qgcs[sl_][c_][0]: e.memset(a_, 0.0), [], [qgcs[sl_][c_][1]])
        attms = [alloc([4, 128], BF16) for _ in range(3)]
        attgs = [alloc([4, 128], BF16) for _ in range(3)]
        osb, b_osb = alloc([D]); obt, b_obt = alloc([D])
        maskc, b_maskc = alloc([512])
        G(lambda e: e.memset(maskc, 1.0), [], [b_maskc])
        zc = 0 if d == 0 else 63
        G(lambda e: e.memset(maskc.rearrange("p (c j) -> p c j", j=64)[:, :, zc:zc + 1], 0.0), [], [b_maskc])
        S_hg, b_Shg = alloc([4, 128]); S_hgb, b_Shgb = alloc([4, 128], BF16)
        S_gl, b_Sgl = alloc([2, 256]); S_glb, b_Sglb = alloc([2, 256], BF16)
        mask = maskf if d == 0 else maskb
        corder = [0, 1] if d == 0 else [1, 0]
        endcol = (lambda c: c * 64 + 63) if d == 0 else (lambda c: c * 64)
        G(lambda e: e.memset(S_hg, 0.0), [], [b_Shg]); G(lambda e: e.memset(S_hgb, 0.0), [], [b_Shgb])
        G(lambda e: e.memset(S_gl, 0.0), [], [b_Sgl]); G(lambda e: e.memset(S_glb, 0.0), [], [b_Sglb])

        seq = [("c", i) for i in range(2)] + [("x", i) for i in range(nt)]
        if d == 1:
            seq = [("c", 1), ("c", 0)] + [("x", i) for i in range(NT - 1, NT - 1 - nt, -1)]

        def stage1(idx):
            kind, i = seq[idx]
            compute_o = kind == "x"
            s = idx % 2
            s3 = idx % 3
            hT, b_hT = hTs[s]
            EE, b_EE = EEs[s3]; Eg, b_Eg = Egs[s3]
            qdec, b_qdec = qdecs[s3]; qgd, b_qgd = qgds[s3]
            vhg, b_vhg = vhgs[s3]; vgl, b_vgl = vgls[s3]
            ktm, b_ktm = ktms[s3]
            attm, b_attm = attms[s3]; attg, b_attg = attgs[s3]
            t_ = tmps[s]
            junk, b_junk = t_['junk']; xn, b_xn = t_['xn']; tq, b_tq = t_['tq']; qq, b_qq = t_['qq']
            cmh, b_cmh = t_['cmh']
            kk, b_kk = t_['kk']; ff, b_ff = t_['ff']; Ei, b_Ei = t_['Ei']; kinv, b_kinv = t_['kinv']; kend, b_kend = t_['kend']
            qg, b_qg = t_['qg']; kg, b_kg = t_['kg']; eg, b_eg = t_['eg']; cum, b_cum = t_['cum']; Egi, b_Egi = t_['Egi']
            kgi, b_kgi = t_['kgi']; kge, b_kge = t_['kge']; rT, b_rT = t_['rT']
            rows = (ctx if kind == "c" else x)[i * TT:(i + 1) * TT, :]
            front(rows, xts[s][0], xts[s][1], junk, b_junk, xn, b_xn, hT, b_hT, sss[s][0][:, 0:1], sss[s][0][:, 1:2], sss[s][1], 2 if kind == "c" else 0, 3, xn_act=(d == 0))
            yield
            fm_job(1, wsc, b_whf, 512, 4, hT, b_hT)
            A(lambda e: e.activation(out=flat(kk), in_=PS[1][:, :], func=AF.Tanh, scale=0.5), [pb[1]], [b_kk])
            yield
            srows = slice(i * TT, (i + 1) * TT)
            use_load = (d == 0 and kind == "x")
            do_store = (d == 1 and kind == "x")
            if use_load:
                ld("sync", flat(qq), sq_q[srows, :], b_qq)
                ld("sync", flat(qg), sq_g[srows, 0:256], b_qg)
                ld("sync", flat(kg), sq_g[srows, 256:512], b_kg)
                ld("sync", vhg, sq_v[srows, 0:512], b_vhg)
                ld("sync", vgl, sq_v[srows, 512:1024], b_vgl)
            else:
                fm_job(2, wsc, b_whq, 0, 4, hT, b_hT)
                A(lambda e: e.activation(out=flat(tq), in_=PS[2][:, :], func=AF.Tanh, scale=0.5), [pb[2]], [b_tq])
                V(lambda e: e.scalar_tensor_tensor(out=flat(qq), in0=flat(tq), scalar=1.0, in1=PS[2][:, :], op0=ALU.add, op1=ALU.mult), [b_tq, pb[2]], [b_qq])
                if do_store:
                    ld("sync", sq_q[srows, :], flat(qq), b_qq, [b_qq])
            for h in range(4):
                ci = d * 4 + h
                G(lambda e, h=h, ci=ci: e.tensor_scalar(out=ff[:, h, :], in0=kk[:, h, :], scalar1=coef[:, 1, ci:ci + 1], scalar2=coef[:, 0, ci:ci + 1], op0=ALU.mult, op1=ALU.add), [b_kk, b_coef], [b_ff])
            A(lambda e: e.activation(out=ff, in_=ff, func=AF.Ln), [b_ff], [b_ff])
            for h in range(4):
                ci = d * 4 + h
                G(lambda e, h=h, ci=ci: e.tensor_scalar(out=kk[:, h, :], in0=kk[:, h, :], scalar1=coef[:, 2, ci:ci + 1], scalar2=coef[:, 1, ci:ci + 1], op0=ALU.mult, op1=ALU.add), [b_kk, b_coef], [b_kk])
            yield
            if not use_load:
                fm_job(1, wsc, b_wg, 1536, 4, hT, b_hT)
                A(lambda e: e.activation(out=flat(qg), in_=PS[1][:, 0:256], func=AF.Copy), [pb[1]], [b_qg])
                A(lambda e: e.activation(out=flat(kg), in_=PS[1][:, 256:512], func=AF.Copy), [pb[1]], [b_kg])
                if do_store:
                    ld("sync", sq_g[srows, 0:256], flat(qg), b_qg, [b_qg])
                    ld("sync", sq_g[srows, 256:512], flat(kg), b_kg, [b_kg])
            if d == 0:
                V(lambda e: e.tensor_tensor_scan(out=flat(cmh), data0=maskc, data1=flat(ff), initial=0.0, op0=ALU.mult, op1=ALU.add), [b_ff, b_maskc], [b_cmh])
            else:
                V(lambda e: e.tensor_tensor_scan(out=flat(cmh)[:, ::-1], data0=maskc[:, ::-1], data1=flat(ff)[:, ::-1], initial=0.0, op0=ALU.mult, op1=ALU.add), [b_ff, b_maskc], [b_cmh])
            A(lambda e: e.activation(out=EE, in_=cmh, func=AF.Exp), [b_cmh], [b_EE])
            A(lambda e: e.activation(out=Ei, in_=cmh, func=AF.Exp, scale=-1.0), [b_cmh], [b_Ei])
            yield
            for k in range(8):
                M(lambda e, k=k: e.matmul(PS[2][0:16, 0:128], lhsT=wsc[:, k, 2560:2576], rhs=hT[:, k, :], start=(k == 0), stop=(k == 7)), [b_wgr, b_hT], [pb[2]], mode=(128, 32))
            A(lambda e: e.activation(out=rT, in_=PS[2][0:16, 0:128], func=AF.Copy), [pb[2]], [b_rT])
            for t in range(2):
                M(lambda e, t=t: e.matmul(PS[2][:, 128 + t * 128:256 + t * 128], lhsT=w2sb[:, d, t * 128:(t + 1) * 128], rhs=rT, start=(t == 0), stop=True, skip_group_check=True), [b_w2, b_rT], [pb[2]], mode=(32, 128))
            for t in range(2):
                A(lambda e, t=t: e.activation(out=eg[:, t, :], in_=PS[2][:, 128 + t * 128:256 + t * 128], func=AF.Exp, scale=-1.0, bias=nb2[:, d * 2 + t:d * 2 + t + 1]), [pb[2], b_nb2], [b_eg])
            A(lambda e: e.activation(out=eg, in_=eg, func=AF.Ln, bias=1.0), [b_eg], [b_eg])
            V(lambda e: e.scalar_tensor_tensor(out=qdec, in0=qq, scalar=0.5 * 128 ** -0.5, in1=EE, op0=ALU.mult, op1=ALU.mult), [b_qq, b_EE], [b_qdec])
            for c in range(2):
                sl = slice(c * 64, (c + 1) * 64)
                qa, qb_ = qdcs[s3][c]
                V(lambda e, sl=sl, qa=qa: e.scalar_tensor_tensor(out=qa[:, :, sl], in0=qq[:, :, sl], scalar=0.5 * 128 ** -0.5, in1=EE[:, :, sl], op0=ALU.mult, op1=ALU.mult), [b_qq, b_EE], [qb_])
            V(lambda e: e.tensor_tensor(out=kinv, in0=kk, in1=Ei, op=ALU.mult), [b_kk, b_Ei], [b_kinv])
            ec0 = endcol(0)
            V(lambda e: e.tensor_tensor(out=kend.rearrange("p h (c j) -> p h c j", j=64), in0=kinv.rearrange("p h (c j) -> p h c j", j=64),
                                        in1=EE.rearrange("p h (c j) -> p h c j", j=64)[:, :, :, ec0:ec0 + 1].to_broadcast([128, 4, 2, 64]), op=ALU.mult), [b_kinv, b_EE], [b_kend])
            yield
            if not use_load:
                tm_job(1, wsc, b_whi, 1024, hT, b_hT)
                A(lambda e: e.activation(out=vhg, in_=PS[1][:, :], func=AF.Copy), [pb[1]], [b_vhg])
                if do_store:
                    ld("sync", sq_v[srows, 0:512], vhg, b_vhg, [b_vhg])
            if d == 0:
                V(lambda e: e.tensor_tensor_scan(out=flat(cum), data0=maskc[:, 0:256], data1=flat(eg), initial=0.0, op0=ALU.mult, op1=ALU.add), [b_eg, b_maskc], [b_cum])
            else:
                V(lambda e: e.tensor_tensor_scan(out=flat(cum)[:, ::-1], data0=maskc[:, 0:256][:, ::-1], data1=flat(eg)[:, ::-1], initial=0.0, op0=ALU.mult, op1=ALU.add), [b_eg, b_maskc], [b_cum])
            A(lambda e: e.activation(out=Eg, in_=cum, func=AF.Exp, scale=-1.0 / 16), [b_cum], [b_Eg])
            A(lambda e: e.activation(out=Egi, in_=cum, func=AF.Exp, scale=1.0 / 16), [b_cum], [b_Egi])
            yield
            if not use_load:
                tm_job(2, wsc, b_wg, 2048, hT, b_hT)
                A(lambda e: e.activation(out=vgl, in_=PS[2][:, :], func=AF.Copy), [pb[2]], [b_vgl])
                if do_store:
                    ld("sync", sq_v[srows, 512:1024], vgl, b_vgl, [b_vgl])
            V(lambda e: e.scalar_tensor_tensor(out=qgd, in0=qg, scalar=0.125, in1=Eg, op0=ALU.mult, op1=ALU.mult), [b_qg, b_Eg], [b_qgd])
            for c in range(2):
                sl = slice(c * 64, (c + 1) * 64)
                qa, qb_ = qgcs[s3][c]
                V(lambda e, sl=sl, qa=qa: e.scalar_tensor_tensor(out=qa[:, :, sl], in0=qg[:, :, sl], scalar=0.125, in1=Eg[:, :, sl], op0=ALU.mult, op1=ALU.mult), [b_qg, b_Eg], [qb_])
            V(lambda e: e.tensor_tensor(out=kgi, in0=kg, in1=Egi, op=ALU.mult), [b_kg, b_Egi], [b_kgi])
            V(lambda e: e.tensor_tensor(out=kge.rearrange("p h (c j) -> p h c j", j=64), in0=kgi.rearrange("p h (c j) -> p h c j", j=64),
                                        in1=Eg.rearrange("p h (c j) -> p h c j", j=64)[:, :, :, ec0:ec0 + 1].to_broadcast([128, 2, 2, 64]), op=ALU.mult), [b_kgi, b_Eg], [b_kge])
            yield
            for h in range(4):
                M(lambda e, h=h: e.transpose(PSb[3][:, h * 128:(h + 1) * 128], kend[:, h, :], identb), [b_kend, b_identb], [pb[3]])
            for t in range(2):
                M(lambda e, t=t: e.transpose(PSb[3][:, 512 + t * 128:640 + t * 128], kge[:, t, :], identb), [b_kge, b_identb], [pb[3]])
            A(lambda e: e.activation(out=ktm[:, 0, :], in_=PSb[3][:, 0:768], func=AF.Copy, scale=maskb[:, 0:1]), [pb[3], b_cst], [b_ktm])
            A(lambda e: e.activation(out=ktm[:, 1, :], in_=PSb[3][:, 0:768], func=AF.Copy, scale=maskf[:, 127:128]), [pb[3], b_cst], [b_ktm])
            if compute_o:
                for h in range(4):
                    M(lambda e, h=h: e.matmul(PS[4][:, h * 128:(h + 1) * 128], lhsT=kinv[:, h, :], rhs=qdec[:, h, :], start=(h == 0), stop=True, skip_group_check=True), [b_kinv, b_qdec], [pb[4]])
                V(lambda e: e.tensor_tensor(out=attm, in0=PS[4][:, :].rearrange("p (h i) -> p h i", h=4), in1=mask.unsqueeze(1).to_broadcast([128, 4, 128]), op=ALU.mult), [pb[4], b_cst], [b_attm])
                yield
                for h in (0, 2, 1, 3):
                    p0 = (h % 2) * 64
                    t = h // 2
                    M(lambda e, h=h, p0=p0, t=t: e.matmul(PS[4][:, h * 128:(h + 1) * 128], lhsT=kgi[p0:p0 + 64, t, :], rhs=qgd[p0:p0 + 64, t, :], start=(h == 0), stop=True, skip_group_check=True), [b_kgi, b_qgd], [pb[4]], mode=(64, 128, p0))
                V(lambda e: e.tensor_tensor(out=attg, in0=PS[4][:, :].rearrange("p (h i) -> p h i", h=4), in1=mask.unsqueeze(1).to_broadcast([128, 4, 128]), op=ALU.mult), [pb[4], b_cst], [b_attg])
            yield

        def stage2(idx):
            kind, i = seq[idx]
            compute_o = kind == "x"
            s = idx % 3
            EE, b_EE = EEs[s]; Eg, b_Eg = Egs[s]
            qdec, b_qdec = qdecs[s]; qgd, b_qgd = qgds[s]
            vhg, b_vhg = vhgs[s]; vgl, b_vgl = vgls[s]
            ktm, b_ktm = ktms[s]
            attm, b_attm = attms[s]; attg, b_attg = attgs[s]
            if compute_o:
                for h in range(4):
                    M(lambda e, h=h: e.matmul(PS[7][:, h * 128:(h + 1) * 128], lhsT=attm[:, h, :], rhs=vhg[:, h * 128:(h + 1) * 128], start=(h == 0), stop=False, skip_group_check=True), [b_attm, b_vhg], [pb[7]])
                for h in range(4):
                    M(lambda e, h=h: e.matmul(PS[0][:, h * 128:(h + 1) * 128], lhsT=attg[:, h, :], rhs=vgl[:, h * 128:(h + 1) * 128], start=(h == 0), stop=False, skip_group_check=True), [b_attg, b_vgl], [pb[0]])
                yield
            for c in corder:
                ts = slice(c * 64, (c + 1) * 64)
                ec = endcol(c)
                if compute_o:
                    qa, qb_ = qdcs[s][c]
                    ga_, gb_ = qgcs[s][c]
                    for h in range(4):
                        M(lambda e, h=h, qa=qa: e.matmul(PS[7][:, h * 128:(h + 1) * 128], lhsT=qa[:, h, :], rhs=S_hgb[:, h, :], start=False, stop=True, skip_group_check=True), [qb_, b_Shgb], [pb[7]])
                    for t in range(2):
                        M(lambda e, t=t, ga_=ga_: e.matmul(PS[0][:, t * 256:(t + 1) * 256], lhsT=ga_[:, t, :], rhs=S_glb[:, t, :], start=False, stop=True, skip_group_check=True), [gb_, b_Sglb], [pb[0]])
                for h in range(4):
                    M(lambda e, h=h, c=c: e.matmul(PS[5][:, h * 128:(h + 1) * 128], lhsT=ktm[:, c, h * 128:(h + 1) * 128], rhs=vhg[:, h * 128:(h + 1) * 128], start=(h == 0), stop=True, skip_group_check=True), [b_ktm, b_vhg], [pb[5]])
                for t in range(2):
                    M(lambda e, t=t, c=c: e.matmul(PS[6][:, t * 256:(t + 1) * 256], lhsT=ktm[:, c, 512 + t * 128:640 + t * 128], rhs=vgl[:, t * 256:(t + 1) * 256], start=(t == 0), stop=True, skip_group_check=True), [b_ktm, b_vgl], [pb[6]])
                yield
                for h in range(4):
                    V(lambda e, h=h, ec=ec: e.scalar_tensor_tensor(out=S_hg[:, h, :], in0=S_hg[:, h, :], scalar=EE[:, h, ec:ec + 1], in1=PS[5][:, h * 128:(h + 1) * 128], op0=ALU.mult, op1=ALU.add), [b_Shg, b_EE, pb[5]], [b_Shg])
                G(lambda e: e.tensor_copy(out=S_hgb, in_=S_hg), [b_Shg], [b_Shgb])
                for t in range(2):
                    for hh in range(2):
                        ps_ = slice(hh * 64, (hh + 1) * 64)
                        cs = slice(hh * 128, (hh + 1) * 128)
                        V(lambda e, t=t, ps_=ps_, cs=cs, ec=ec: e.scalar_tensor_tensor(out=S_gl[ps_, t, cs], in0=S_gl[ps_, t, cs], scalar=Eg[ps_, t, ec:ec + 1], in1=PS[6][ps_, t * 256 + cs.start:t * 256 + cs.stop], op0=ALU.mult, op1=ALU.add), [b_Sgl, b_Eg, pb[6]], [b_Sgl])
                G(lambda e: e.tensor_copy(out=S_glb, in_=S_gl), [b_Sgl], [b_Sglb])
                yield
            if compute_o:
                rows = slice(i * TT, (i + 1) * TT)
                if d == 1:
                    A(lambda e: e.activation(out=osb[:, 0:512], in_=PS[7][:, :], func=AF.Copy), [pb[7]], [b_osb])
                    V(lambda e: e.tensor_copy(out=osb[:, 512:1024], in_=PS[0][:, :]), [pb[0]], [b_osb])
                    ld("sync", obw[rows, :], osb, b_osb, [b_osb])
                else:
                    ld("sync", obt, obw[rows, :], b_obt)
                    V(lambda e: e.tensor_tensor(out=osb[:, 0:512], in0=PS[7][:, :], in1=obt[:, 0:512], op=ALU.add), [pb[7], b_obt], [b_osb])
                    V(lambda e: e.tensor_tensor(out=osb[:, 512:1024], in0=PS[0][:, :], in1=obt[:, 512:1024], op=ALU.add), [pb[0], b_obt], [b_osb])
                    ld("sync", osum[rows, :], osb, b_osb, [b_osb])
            yield

        def interleave(*gens):
            gens = list(gens)
            while gens:
                for g_ in list(gens):
                    try:
                        next(g_)
                    except StopIteration:
                        gens.remove(g_)

        n_ = len(seq)
        H1 = 4
        g1 = {}
        for r in range(n_ + 2):
            items = []
            if 0 <= r - 2 < n_:
                items.append((stage2(r - 2), None))
            if 0 <= r - 1 < n_:
                items.append((g1.pop(r - 1), None))
            if r < n_:
                g1[r] = stage1(r)
                items.append((g1[r], H1))
            run_round(items)
        P.barrier()

    def interleave(*gens):
        gens = list(gens)
        while gens:
            for g_ in list(gens):
                try:
                    next(g_)
                except StopIteration:
                    gens.remove(g_)

    def out_pass(nt):
        ptr[0] = persist_end
        P.pe_skip = True
        wg, _ = alloc([8, 3072], BF16)
        wi = w_in.rearrange("(k p) c -> p k c", p=128)
        bwg = []
        for (dst, src, n) in [(0, C_HOG, 512), (512, C_GOG, 512), (1024, C_GA, 1024), (2048, C_GB, 1024)]:
            b_ = Buf(f"wg{dst}")
            bwg.append(b_)
            P.op("gpsimd", lambda e, dst=dst, src=src, n=n: e.dma_start(out=wg[:, :, dst:dst + n], in_=wi[:, :, src:src + n]), [], [b_], dma=b_)
        wab, b_wab = alloc([8, D], BF16)
        wo, b_wo = alloc([8, D], BF16)
        P.op("gpsimd", lambda e: e.dma_start(out=wo, in_=w_o.rearrange("(k p) c -> p k c", p=128)), [], [b_wo], dma=Buf("wo"))
        wst_off = ptr[0]
        wst, b_wst = alloc([4, D])
        for j, wsrc in enumerate([w_a, w_b]):
            ld("sync", wst, wsrc.rearrange("(h p) c -> p h c", p=128), b_wst)
            V(lambda e, j=j: e.tensor_scalar(out=wab[:, 4 * j:4 * j + 4, :], in0=wst, scalar1=fmp[:, FM_ON + j:FM_ON + j + 1], scalar2=None, op0=ALU.mult), [b_wst, b_fmp], [b_wab])
        P.barrier()
        ptr[0] = wst_off
        xts = [alloc([D]) for _ in range(3)]
        hTs = [alloc([8, 128], BF16) for _ in range(2)]
        sss = [alloc([2]) for _ in range(2)]
        junk, b_junk = alloc([D], BF16)
        xn, b_xn = alloc([D], BF16)
        tg, b_tg = alloc([D])
        sogs = [alloc([D]) for _ in range(3)]
        thas = [alloc([D], BF16) for _ in range(3)]
        thbs = [alloc([D], BF16) for _ in range(3)]
        tB = []
        for _ in range(2):
            t_ = {}
            t_['os'] = alloc([D]); t_['sq'] = alloc([D]); t_['ssq'] = alloc([16])
            t_['yy'] = alloc([D], BF16); t_['yT'] = alloc([8, 128], BF16)
            t_['m1'] = alloc([D]); t_['m2'] = alloc([D]); t_['mT'] = alloc([8, 128], BF16)
            t_['mtm'] = alloc([D], BF16); t_['x1'] = alloc([D]); t_['ssx'] = alloc([4])
            tB.append(t_)
        print("out_pass sbuf words", ptr[0])

        def stageA(i):
            s = i % 2
            s3 = i % 3
            hT, b_hT = hTs[s]
            sog, b_sog = sogs[s3]; tha, b_tha = thas[s3]; thb, b_thb = thbs[s3]
            front(x[i * TT:(i + 1) * TT, :], xts[s3][0], xts[s3][1], junk, b_junk, xn, b_xn, hT, b_hT, sss[s][0][:, 0:1], sss[s][0][:, 1:2], sss[s][1], 0, 0)
            yield
            for j in range(2):
                bk = 1 + j
                tm_job(bk, wg, bwg[j], 512 * j, hT, b_hT)
                A(lambda e, j=j, bk=bk: e.activation(out=tg[:, j * 512:(j + 1) * 512], in_=PS[bk][:, :], func=AF.Tanh, scale=0.5), [pb[bk]], [b_tg])
                V(lambda e, j=j, bk=bk: e.scalar_tensor_tensor(out=sog[:, j * 512:(j + 1) * 512], in0=tg[:, j * 512:(j + 1) * 512], scalar=1.0, in1=PS[bk][:, :], op0=ALU.add, op1=ALU.mult), [b_tg, pb[bk]], [b_sog])
            yield
            for j in range(4):
                bk = 1 + j % 2
                tm_job(bk, wg, bwg[2 + j // 2], 1024 + 512 * j, hT, b_hT)
                dst = (tha if j < 2 else thb)[:, (j % 2) * 512:(j % 2 + 1) * 512]
                bd = b_tha if j < 2 else b_thb
                A(lambda e, dst=dst, bk=bk: e.activation(out=dst, in_=PS[bk][:, :], func=AF.Tanh, scale=0.5), [pb[bk]], [bd])
                if j % 2 == 1:
                    yield

        def stageB(i):
            s = i % 3
            xt, b_xt = xts[s]
            sog, b_sog = sogs[s]; tha, b_tha = thas[s]; thb, b_thb = thbs[s]
            t_ = tB[i % 2]
            os_, b_os = t_['os']; sq, b_sq = t_['sq']; ssq, b_ssq = t_['ssq']; yy, b_yy = t_['yy']; yT, b_yT = t_['yT']
            m1, b_m1 = t_['m1']; m2, b_m2 = t_['m2']; mT, b_mT = t_['mT']; mtm, b_mtm = t_['mtm']; x1, b_x1 = t_['x1']
            ssx, b_ssx = t_['ssx']
            rows = slice(i * TT, (i + 1) * TT)
            ld("sync", os_, osum[rows, :], b_os)
            A(lambda e: e.activation(out=sq, in_=os_, func=AF.Square), [b_os], [b_sq])
            V(lambda e: e.tensor_reduce(out=ssq[:, 0:8], in_=sq.rearrange("p (h v) -> p h v", h=8), axis=AX.X, op=ALU.add), [b_sq], [b_ssq])
            G(lambda e: e.tensor_scalar(out=ssq[:, 0:8], in0=ssq[:, 0:8], scalar1=1.0 / 128, scalar2=EPS, op0=ALU.mult, op1=ALU.add), [b_ssq], [b_ssq])
            G(lambda e: e.tensor_tensor(out=ssq[:, 8:16], in0=ssq[:, 0:8], in1=mhalf, op=ALU.pow), [b_ssq, b_mhalf], [b_ssq])
            yield
            for h in range(8):
                V(lambda e, h=h: e.scalar_tensor_tensor(out=yy[:, h * 128:(h + 1) * 128], in0=os_[:, h * 128:(h + 1) * 128], scalar=ssq[:, 8 + h:9 + h], in1=sog[:, h * 128:(h + 1) * 128], op0=ALU.mult, op1=ALU.mult), [b_os, b_ssq, b_sog], [b_yy])
            for h in range(8):
                M(lambda e, h=h: e.transpose(PSb[3][:, h * 128:(h + 1) * 128], yy[:, h * 128:(h + 1) * 128], identb), [b_yy, b_identb], [pb[3]])
            A(lambda e: e.activation(out=yT, in_=PSb[3][:, :].rearrange("p (h t) -> p h t", h=8), func=AF.Copy), [pb[3]], [b_yT])
            yield
            for j, (bk0, mm, bm, th, bth) in enumerate([(4, m1, b_m1, tha, b_tha), (6, m2, b_m2, thb, b_thb)]):
                for hf in range(2):
                    bank = bk0 + hf
                    for kc in range(4):
                        M(lambda e, bank=bank, hf=hf, kc=kc, j=j: e.matmul(PS[bank][:, :], lhsT=yT[:, 4 * j + kc, :], rhs=wab[:, 4 * j + kc, hf * 512:(hf + 1) * 512], start=(kc == 0), stop=(kc == 3)), [b_wab, b_yT], [pb[bank]])
                    V(lambda e, hf=hf, bank=bank, mm=mm, th=th: e.scalar_tensor_tensor(out=mm[:, hf * 512:(hf + 1) * 512], in0=th[:, hf * 512:(hf + 1) * 512], scalar=1.0, in1=PS[bank][:, :], op0=ALU.add, op1=ALU.mult), [bth, pb[bank]], [bm])
                yield
            G(lambda e: e.tensor_tensor(out=mtm, in0=m1, in1=m2, op=ALU.add), [b_m1, b_m2], [b_mtm])
            for k in range(8):
                M(lambda e, k=k: e.transpose(PSb[3][:, k * 128:(k + 1) * 128], mtm[:, k * 128:(k + 1) * 128], identb), [b_mtm, b_identb], [pb[3]])
            A(lambda e: e.activation(out=mT, in_=PSb[3][:, :].rearrange("p (h t) -> p h t", h=8), func=AF.Copy), [pb[3]], [b_mT])
            yield
            for hf in range(2):
                for k in range(8):
                    M(lambda e, hf=hf, k=k: e.matmul(PS[4 + hf][:, :], lhsT=mT[:, k, :], rhs=wo[:, k, hf * 512:(hf + 1) * 512], start=(k == 0), stop=(k == 7)), [b_mT, b_wo], [pb[4 + hf]])
                A(lambda e, hf=hf: e.activation(out=sq[:, hf * 512:(hf + 1) * 512], in_=PS[4 + hf][:, :], func=AF.Square, accum_out=ssx[:, hf:hf + 1]), [pb[4 + hf]], [b_sq, b_ssx])
            G(lambda e: e.tensor_tensor(out=ssx[:, 2:3], in0=ssx[:, 0:1], in1=ssx[:, 1:2], op=ALU.add), [b_ssx], [b_ssx])
            G(lambda e: e.tensor_scalar(out=ssx[:, 2:3], in0=ssx[:, 2:3], scalar1=1.0 / D, scalar2=16 * EPS, op0=ALU.mult, op1=ALU.add), [b_ssx], [b_ssx])
            G(lambda e: e.tensor_tensor(out=ssx[:, 3:4], in0=ssx[:, 2:3], in1=mhalf[:, 0:1], op=ALU.pow), [b_ssx, b_mhalf], [b_ssx])
            for hf in range(2):
                V(lambda e, hf=hf: e.scalar_tensor_tensor(out=x1[:, hf * 512:(hf + 1) * 512], in0=PS[4 + hf][:, :], scalar=ssx[:, 3:4], in1=Gt[:, 0, hf * 512:(hf + 1) * 512], op0=ALU.mult, op1=ALU.mult), [pb[4 + hf], b_ssx, b_Gt], [b_x1])
            G(lambda e, xt=xt: e.tensor_tensor(out=x1, in0=x1, in1=xt, op=ALU.add), [b_x1, b_xt], [b_x1])
            ld("sync", x1s[rows, :], x1, b_x1, [b_x1])
            yield

        HB = 3
        gB = {}
        for r in range(nt + 2):
            items = []
            if 0 <= r - 2 < nt:
                items.append((gB.pop(r - 2), None))
            if 0 <= r - 1 < nt:
                gB[r - 1] = stageB(r - 1)
                items.append((gB[r - 1], HB))
            if r < nt:
                items.append((stageA(r), None))
            run_round(items)
        P.barrier()

    def ffn_pass(nt):
        ptr[0] = persist_end
        P.pe_skip = True
        wup, b_wup = alloc([8, 2 * FH], BF16)
        bwu = []
        for j in range(4):
            b_ = Buf(f"wup{j}")
            bwu.append(b_)
            P.op("gpsimd", lambda e, j=j: e.dma_start(out=wup[:, :, j * 1408:(j + 1) * 1408], in_=w_up.rearrange("(k p) c -> p k c", p=128)[:, :, j * 1408:(j + 1) * 1408]), [], [b_], dma=b_)
        wdn, _ = alloc([NFB, D], BF16)
        bwd = []
        for j in range(2):
            b_ = Buf(f"wdn{j}")
            bwd.append(b_)
            P.op("gpsimd", lambda e, j=j: e.dma_start(out=wdn[:, j * 11:(j + 1) * 11, :], in_=w_dn.rearrange("(f p) c -> p f c", p=128)[:, j * 11:(j + 1) * 11, :]), [], [b_], dma=b_)
        xf, b_xf = alloc([D])
        res, b_res = xf, b_xf
        junk, b_junk = alloc([D], BF16)
        xn, b_xn = alloc([D], BF16)
        h2Ts = [alloc([8, 640], BF16) for _ in range(2)]
        ssc, b_ssc = alloc([2])
        usbs = [alloc([10, 66], BF16) for _ in range(2)]
        dgs = [alloc([9, 128], BF16) for _ in range(2)]
        gl, b_gl = alloc([512], BF16)
        aT, b_aT = alloc([NFB, 512], BF16)
        tmp, b_tmp = alloc([512])
        cw = fmp[:, FM_CW:FM_CW + 198].rearrange("p (f t) -> p f t", t=9)
        for u_ in usbs:
            G(lambda e, u_=u_: e.memset(u_[0], 0.0), [], [u_[1]])
        last = [None]

        def wub(col):
            return bwu[col // 1408]

        def frontgen(it):
            t0 = it * 512
            h2T, b_h2T = h2Ts[it % 2]
            for s in range(5):
                c0 = s * 128
                if s < 4:
                    ld("sync", xf, x1s[t0 + s * 128:t0 + (s + 1) * 128, :], b_xf)
                else:
                    G(lambda e: e.memset(xf, 0.0), [], [b_xf])
                    if it > 0:
                        ld("sync", xf[0:64, :], x1s[t0 - 64:t0, :], b_xf)
                    if it < nt - 1:
                        ld("sync", xf[64:128, :], x1s[t0 + 512:t0 + 576, :], b_xf)
                A(lambda e: e.activation(out=junk, in_=xf, func=AF.Square, accum_out=ssc[:, 0:1]), [b_xf], [b_junk, b_ssc])
                G(lambda e: e.tensor_scalar(out=ssc[:, 0:1], in0=ssc[:, 0:1], scalar1=1.0 / D, scalar2=EPS, op0=ALU.mult, op1=ALU.add), [b_ssc], [b_ssc])
                G(lambda e: e.tensor_tensor(out=ssc[:, 1:2], in0=ssc[:, 0:1], in1=mhalf[:, 0:1], op=ALU.pow), [b_ssc, b_mhalf], [b_ssc])
                A(lambda e: e.activation(out=xn, in_=xf, func=AF.Copy, scale=ssc[:, 1:2]), [b_xf, b_ssc], [b_xn])
                for k in range(8):
                    M(lambda e, k=k: e.transpose(PSb[0][:, k * 128:(k + 1) * 128], xn[:, k * 128:(k + 1) * 128], identb), [b_xn, b_identb], [pb[0]])
                for k in range(8):
                    if k % 2 == 0:
                        A(lambda e, k=k, c0=c0: e.activation(out=h2T[:, k, c0:c0 + 128], in_=PSb[0][:, k * 128:(k + 1) * 128], func=AF.Identity, scale=GB[:, 4, k:k + 1], bias=GB[:, 5, k:k + 1]), [pb[0], b_GB], [b_h2T])
                    else:
                        V(lambda e, k=k, c0=c0: e.tensor_scalar(out=h2T[:, k, c0:c0 + 128], in0=PSb[0][:, k * 128:(k + 1) * 128], scalar1=GB[:, 4, k:k + 1], scalar2=GB[:, 5, k:k + 1], op0=ALU.mult, op1=ALU.add), [pb[0], b_GB], [b_h2T])
                yield
            if it == 0:
                G(lambda e: e.memset(h2T[:, :, 512:576], 0.0), [], [b_h2T])
            if it == nt - 1:
                G(lambda e: e.memset(h2T[:, :, 576:640], 0.0), [], [b_h2T])
            yield

        def body(it):
            t0 = it * 512
            h2T, b_h2T = h2Ts[it % 2]

            def conv(fb):
                usb, b_usb = usbs[fb % 2]
                dg, b_dg = dgs[fb % 2]
                vb = 3 + (fb % 2)
                pe_taps = [0, 1, 2, 6, 7, 8]
                for tap in pe_taps:
                    dr, dc = tap // 3 - 1, tap % 3 - 1
                    M(lambda e, tap=tap, dr=dr, dc=dc: e.matmul(PS[6][:, :], lhsT=dg[:, tap, :], rhs=usb[:, 1 + dr:9 + dr, 1 + dc:65 + dc], start=(tap == pe_taps[0]), stop=(tap == pe_taps[-1])), [b_dg, b_usb], [pb[6]])
                acc3 = tmp.rearrange("p (r c) -> p r c", r=8)
                V(lambda e: e.tensor_scalar(out=acc3, in0=usb[:, 1:9, 0:64], scalar1=cw[:, fb, 3:4], scalar2=None, op0=ALU.mult), [b_usb, b_fmp], [b_tmp])
                V(lambda e: e.scalar_tensor_tensor(out=acc3, in0=usb[:, 1:9, 1:65], scalar=cw[:, fb, 4:5], in1=acc3, op0=ALU.mult, op1=ALU.add), [b_usb, b_fmp, b_tmp], [b_tmp])
                V(lambda e: e.scalar_tensor_tensor(out=acc3, in0=usb[:, 1:9, 2:66], scalar=cw[:, fb, 5:6], in1=acc3, op0=ALU.mult, op1=ALU.add), [b_usb, b_fmp, b_tmp], [b_tmp])
                V(lambda e: e.tensor_tensor(out=tmp, in0=PS[6][:, :], in1=tmp, op=ALU.add), [pb[6], b_tmp], [b_tmp])
                A(lambda e: e.activation(out=gl, in_=tmp, func=AF.Gelu_apprx_tanh, bias=fmp[:, FM_CB + fb:FM_CB + fb + 1]), [b_tmp, b_fmp], [b_gl])
                V(lambda e: e.tensor_tensor(out=aT[:, fb, :], in0=gl, in1=PS[vb][:, :], op=ALU.mult), [b_gl, pb[vb]], [b_aT])

            for fb in range(NFB):
                ub, vb = 1 + (fb % 2), 3 + (fb % 2)
                usb, b_usb = usbs[fb % 2]
                dg, b_dg = dgs[fb % 2]
                bu = wub(fb * 128)
                bv = wub(FH + fb * 128)
                for k in range(8):
                    M(lambda e, k=k, ub=ub, fb=fb: e.matmul(PS[ub][:, :], lhsT=wup[:, k, fb * 128:(fb + 1) * 128], rhs=h2T[:, k, 0:512], start=(k == 0), stop=(k == 7)), [bu, b_h2T], [pb[ub]])
                for k in range(8):
                    M(lambda e, k=k, fb=fb: e.matmul(PS[5][:, 0:128], lhsT=wup[:, k, fb * 128:(fb + 1) * 128], rhs=h2T[:, k, 512:640], start=(k == 0), stop=(k == 7)), [bu, b_h2T], [pb[5]])
                A(lambda e, ub=ub, usb=usb: e.activation(out=usb[:, 1:9, 1:65], in_=PS[ub][:, :].rearrange("p (r c) -> p r c", r=8), func=AF.Copy), [pb[ub]], [b_usb])
                A(lambda e, usb=usb: e.activation(out=usb[:, 0:10:9, 1:65], in_=PS[5][:, 0:128].rearrange("p (r c) -> p r c", r=2), func=AF.Copy), [pb[5]], [b_usb])
                G(lambda e, dg=dg, fb=fb: e.tensor_tensor(out=dg, in0=identb.unsqueeze(1).to_broadcast([128, 9, 128]), in1=cw[:, fb, :].unsqueeze(2).to_broadcast([128, 9, 128]), op=ALU.mult), [b_identb, b_fmp], [b_dg])
                if fb > 0:
                    conv(fb - 1)
                for k in range(8):
                    M(lambda e, k=k, vb=vb, fb=fb: e.matmul(PS[vb][:, :], lhsT=wup[:, k, FH + fb * 128:FH + (fb + 1) * 128], rhs=h2T[:, k, 0:512], start=(k == 0), stop=(k == 7)), [bv, b_h2T], [pb[vb]])
                yield
            conv(NFB - 1)
            yield
            for tt in range(4):
                for hf in range(2):
                    bank = 7 if hf == 0 else 1
                    for fb in range(NFB):
                        M(lambda e, tt=tt, hf=hf, fb=fb, bank=bank: e.matmul(PS[bank][:, :], lhsT=aT[:, fb, tt * 128:(tt + 1) * 128], rhs=wdn[:, fb, hf * 512:(hf + 1) * 512], start=(fb == 0), stop=(fb == NFB - 1)), [b_aT, bwd[fb // 11]], [pb[bank]])
                    A(lambda e, hf=hf, bank=bank: e.activation(out=junk[:, hf * 512:(hf + 1) * 512], in_=PS[bank][:, :], func=AF.Square, accum_out=ssx[:, hf:hf + 1]), [pb[bank]], [b_junk, b_ssx])
                G(lambda e: e.tensor_tensor(out=ssx[:, 2:3], in0=ssx[:, 0:1], in1=ssx[:, 1:2], op=ALU.add), [b_ssx], [b_ssx])
                G(lambda e: e.tensor_scalar(out=ssx[:, 2:3], in0=ssx[:, 2:3], scalar1=1.0 / D, scalar2=EPS, op0=ALU.mult, op1=ALU.add), [b_ssx], [b_ssx])
                G(lambda e: e.tensor_tensor(out=ssx[:, 3:4], in0=ssx[:, 2:3], in1=mhalf[:, 0:1], op=ALU.pow), [b_ssx, b_mhalf], [b_ssx])
                ld("sync", res, x1s[t0 + tt * 128:t0 + (tt + 1) * 128, :], b_res)
                for hf in range(2):
                    bank = 7 if hf == 0 else 1
                    V(lambda e, hf=hf, bank=bank: e.scalar_tensor_tensor(out=tmp, in0=PS[bank][:, :], scalar=ssx[:, 3:4], in1=Gt[:, 1, hf * 512:(hf + 1) * 512], op0=ALU.mult, op1=ALU.mult), [pb[bank], b_ssx, b_Gt], [b_tmp])
                    G(lambda e, hf=hf: e.tensor_tensor(out=res[:, hf * 512:(hf + 1) * 512], in0=res[:, hf * 512:(hf + 1) * 512], in1=tmp, op=ALU.add), [b_res, b_tmp], [b_res])
                last[0] = ld("sync", out[t0 + tt * 128:t0 + (tt + 1) * 128, :], res, b_res, [b_res])
                yield

        interleave(frontgen(0))
        for it in range(nt):
            if it + 1 < nt:
                interleave(body(it), frontgen(it + 1))
            else:
                interleave(body(it))
        return last[0]

    scan_pass(1, ntiles)
    scan_pass(0, ntiles)
    out_pass(ntiles)
    last = ffn_pass(nffn)
    P.barrier()
    P.emit(final_waits=[o for o in [last] if o is not None])
    st.close()
    return nc


def _prep(inputs):
    f = lambda a: np.ascontiguousarray(np.asarray(a, dtype=np.float32))
    x = f(inputs["x"]); c = f(inputs["c"]); ctx = f(inputs["ctx"]); c_ctx = f(inputs["c_ctx"])
    ada_b = f(inputs["ada_b"])[0]
    fm = lambda v: v.reshape(-1, 128).T
    cst = np.zeros((128, 3, 128), np.float32)
    cst[:, 0, :] = np.eye(128, dtype=np.float32)
    j = np.arange(128)[:, None]; i = np.arange(128)[None, :]
    same = (j // 64) == (i // 64)
    cst[:, 1, :] = (same & (j <= i)).astype(np.float32)
    cst[:, 2, :] = (same & (j >= i)).astype(np.float32)
    rowp = np.concatenate([ada_b[2 * D:3 * D], ada_b[5 * D:6 * D], f(inputs["mix_post_g"])[0], f(inputs["ffn_post_g"])[0]])
    rowp = np.ascontiguousarray(np.broadcast_to(rowp[None, :], (128, 4096)))
    lbl = f(inputs["hg_lb_logits"])
    common = {}
    w2 = np.ascontiguousarray(f(inputs["gla_gk_w2"])[0].transpose(1, 0, 2))
    maps = []
    for b in range(8):
        fmp = np.zeros((128, NFM), np.float32)
        fmp[:, FM_C:FM_C + 8] = fm(c[b])
        fmp[:, FM_CC:FM_CC + 8] = fm(c_ctx)
        for jj, g in enumerate([0, 1, 3, 4]):
            fmp[:, FM_ADAB + 8 * jj:FM_ADAB + 8 * jj + 8] = fm(ada_b[g * D:(g + 1) * D])
        fmp[:, FM_PRE1:FM_PRE1 + 8] = fm(f(inputs["mix_pre_g"])[0])
        fmp[:, FM_PRE2:FM_PRE2 + 8] = fm(f(inputs["ffn_pre_g"])[0])
        fmp[:, FM_LB:FM_LB + 16] = lbl.reshape(2, 2, 4, 128).transpose(3, 0, 1, 2).reshape(128, 16)
        fmp[:, FM_GKB:FM_GKB + 4] = f(inputs["gla_gk_b"])[0].reshape(2, 2, 128).transpose(2, 0, 1).reshape(128, 4)
        fmp[:, FM_ON] = f(inputs["hg_onorm_g"])[0]
        fmp[:, FM_ON + 1] = f(inputs["gla_onorm_g"])[0]
        fmp[:, FM_CW:FM_CW + 198] = f(inputs["ffn_conv_w"])[0].reshape(9, NFB, 128).transpose(2, 1, 0).reshape(128, 198)
        fmp[:, FM_CB:FM_CB + NFB] = fm(f(inputs["ffn_conv_b"])[0])
        maps.append({
            "x": x[b], "ctx": ctx[b], "fmp": fmp, "rowp": rowp, "cst": cst, "w2": w2,
            "ada_w": f(inputs["ada_w"])[0], "w_in": f(inputs["w_in"])[0], "w_a": f(inputs["w_branch_a"])[0],
            "w_b": f(inputs["w_branch_b"])[0], "w_o": f(inputs["w_out"])[0], "w_up": f(inputs["ffn_w_up"])[0],
            "w_dn": f(inputs["ffn_w_down"])[0],
        })
    return maps


def kernel(**inputs):
    maps = _prep(inputs)
    nc = build_nc()
    res = run_bass_kernel_spmd(nc, maps, core_ids=list(range(8)))
    return np.stack([np.asarray(r["out"], dtype=np.float32) for r in res.results], axis=0)
```

```python
import contextlib
import numpy as np
import concourse.bass as bass
import concourse.mybir as mybir
from concourse.bass_utils import run_bass_kernel_spmd

F32 = mybir.dt.float32
BF16 = mybir.dt.bfloat16
AF = mybir.ActivationFunctionType
ALU = mybir.AluOpType
AX = mybir.AxisListType

ENGS = ("tensor", "vector", "scalar", "gpsimd", "sync")
EPOCH = 20000
import os as _os
NOSYNC_ENGS = set(x for x in _os.environ.get('NOSYNC', 'scalar').split(',') if x)

D = 1024
CTX = 256
TT = 128
EPS = 1e-6
FH = 2816
NFB = 22
NB = 53200


class Buf:
    __slots__ = ("name", "last_w", "readers", "dsem", "dcount")

    def __init__(self, name):
        self.name = name
        self.last_w = None
        self.readers = []
        self.dsem = None
        self.dcount = 0


class Op:
    __slots__ = ("eng", "fn", "deps", "signal", "tok", "is_dma", "buf", "qi")

    def __init__(self, eng, fn):
        self.eng = eng
        self.fn = fn
        self.deps = []
        self.signal = False
        self.tok = None
        self.is_dma = False
        self.buf = None


class Prog:
    def __init__(self, nc):
        self.nc = nc
        self.q = {e: [] for e in ENGS}
        self.dma_bufs = []
        self.pe_skip = True
        self.last_pe = None
        self.last_pe_mode = (128, 128)

    def op(self, eng, fn, reads=(), writes=(), dma=None, extra=(), mode=(128, 128)):
        o = Op(eng, fn)
        deps = list(extra)
        force = None
        if eng == "tensor" and fn is not None:
            if mode != self.last_pe_mode and self.last_pe is not None:
                force = self.last_pe
            self.last_pe = o
            self.last_pe_mode = mode
        for b in reads:
            if b.last_w is not None:
                deps.append(b.last_w)
        for b in writes:
            if b.last_w is not None:
                deps.append(b.last_w)
            deps.extend(b.readers)
        if force is not None:
            deps.append(force)
        seen = set()
        for d in deps:
            if self.pe_skip and eng == "tensor" and d.eng == "tensor" and not d.is_dma and d is not force:
                continue
            if id(d) not in seen:
                seen.add(id(d))
                o.deps.append(d)
                d.signal = True
        for b in reads:
            b.readers.append(o)
        for b in writes:
            b.last_w = o
            b.readers = []
        if dma is not None:
            o.is_dma = True
            o.buf = dma
            if dma not in self.dma_bufs:
                self.dma_bufs.append(dma)
        self.q[eng].append(o)
        return o

    def barrier(self):
        lasts = []
        for e in ENGS:
            for o in reversed(self.q[e]):
                if not o.is_dma and o.fn is not None:
                    lasts.append(o)
                    break
        for b in self.dma_bufs:
            for e in ENGS:
                pass
        dl = {}
        for e in ENGS:
            for o in self.q[e]:
                if o.is_dma:
                    dl[id(o.buf)] = o
        lasts.extend(dl.values())
        for e in ENGS:
            self.op(e, None, extra=[l for l in lasts])

    def emit(self, final_waits=()):
        nc = self.nc
        with contextlib.ExitStack() as st:
            esems = {}
            for e in ENGS:
                n = sum(1 for o in self.q[e] if o.signal and not o.is_dma)
                esems[e] = [st.enter_context(nc.semaphore(f"s_{e}_{i}")) for i in range(n // EPOCH + 1)]
            for b in self.dma_bufs:
                b.dsem = st.enter_context(nc.semaphore(f"d_{b.name}"))
                b.dcount = 0
            for e in ENGS:
                c = 0
                for o in self.q[e]:
                    if o.is_dma:
                        o.buf.dcount += 16
                        o.tok = (o.buf.dsem, o.buf.dcount)
                        o.signal = True
                    elif o.signal:
                        if o.fn is None:
                            o.signal = False
                            continue
                        o.tok = (esems[e][c // EPOCH], c % EPOCH + 1)
                        c += 1
            block = st.enter_context(nc.Block())

            for e in ENGS:
                for qi, o in enumerate(self.q[e]):
                    o.qi = qi
            KD = int(_os.environ.get('KDIST', '3'))

            def run(eng_name):
                def body(eng):
                    waited = {}
                    for o in self.q[eng_name]:
                        for d in o.deps:
                            if d.tok is None:
                                continue
                            if eng_name in NOSYNC_ENGS and d.eng == eng_name and not d.is_dma:
                                continue
                            if KD > 0 and eng_name in ('vector', 'scalar') and d.eng == eng_name and not d.is_dma and o.qi - d.qi >= KD:
                                continue
                            s, v = d.tok
                            k = id(s)
                            if waited.get(k, 0) >= v:
                                continue
                            waited[k] = v
                            eng.wait_ge(s, v)
                        if o.fn is None:
                            continue
                        ins = o.fn(eng)
                        if o.signal:
                            s, v = o.tok
                            ins.then_inc(s, 16 if o.is_dma else 1)
                    if eng_name == "sync":
                        for o in final_waits:
                            s, v = o.tok
                            eng.wait_ge(s, v)
                return body

            block.tensor(run("tensor"))
            block.vector(run("vector"))
            block.scalar(run("scalar"))
            block.gpsimd(run("gpsimd"))
            block.sync(run("sync"))


C_HQ, C_HF, C_HI, C_HOG = 0, 512, 1536, 2048
C_GQ, C_GK, C_GV, C_GOG = 2560, 2816, 3072, 3584
C_GR, C_GA, C_GB = 4096, 4128, 5152
IN_W = 6176

FM_C, FM_CC, FM_ADAB, FM_PRE1, FM_PRE2, FM_LB, FM_GKB, FM_ON, FM_CW, FM_CB = 0, 8, 16, 48, 56, 64, 80, 84, 86, 284
NFM = 306


def build_nc(T=8192, debug=False):
    NT = T // TT
    ntiles = NT
    nffn = T // 512
    nc = bass.Bass("TRN2", target_bir_lowering=False)

    def din(name, shape):
        return nc.dram_tensor(name, shape, F32, kind="ExternalInput").ap()

    x = din("x", [T, D])
    ctx = din("ctx", [CTX, D])
    fmp_d = din("fmp", [128, NFM])
    rowp_d = din("rowp", [128, 4096])
    cst_d = din("cst", [128, 3, 128])
    w2_d = din("w2", [16, 2, 256])
    ada_w = din("ada_w", [D, 6 * D])
    w_in = din("w_in", [D, IN_W])
    w_a = din("w_a", [512, D])
    w_b = din("w_b", [512, D])
    w_o = din("w_o", [D, D])
    w_up = din("w_up", [D, 2 * FH])
    w_dn = din("w_dn", [FH, D])
    out = nc.dram_tensor("out", [T, D], F32, kind="ExternalOutput").ap()
    skind = "ExternalOutput" if debug else "Internal"
    obw = nc.dram_tensor("obw", [T, D], F32, kind=skind).ap()
    osum = nc.dram_tensor("osum", [T, D], F32, kind=skind).ap()
    x1s = nc.dram_tensor("x1s", [T, D], F32, kind=skind).ap()
    sq_q = nc.dram_tensor("sq_q", [T, 512], F32, kind="Internal").ap()
    sq_g = nc.dram_tensor("sq_g", [T, 512], F32, kind="Internal").ap()
    sq_v = nc.dram_tensor("sq_v", [T, 1024], BF16, kind="Internal").ap()

    import os
    _ps = os.environ.get('PESKIP', 'setup,scan,out,ffn').split(',')
    P = Prog(nc)
    P.pe_skip = 'setup' in _ps
    st = contextlib.ExitStack()
    big = st.enter_context(nc.sbuf_tensor("big", [128, NB], F32))
    PS = [st.enter_context(nc.psum_tensor(f"ps{i}", [128, 512], F32)) for i in range(8)]
    pb = [Buf(f"ps{i}") for i in range(8)]
    PSb = [p[:].bitcast(BF16) for p in PS]

    ptr = [0]
    nbuf = [0]

    def alloc(shape, dt=F32, parts=128):
        n = int(np.prod(shape))
        words = n if dt == F32 else (n + 1) // 2
        off = ptr[0]
        ptr[0] += words
        assert ptr[0] <= NB, ("SBUF overflow", ptr[0])
        ap = big[0:parts, off:off + words]
        if dt != F32:
            ap = ap.bitcast(dt)[:, 0:n]
        if len(shape) == 2:
            ap = ap.rearrange("p (a b) -> p a b", a=shape[0])
        elif len(shape) == 3:
            ap = ap.rearrange("p (a b c) -> p a b c", a=shape[0], b=shape[1])
        nbuf[0] += 1
        return ap, Buf(f"b{nbuf[0]}")

    dumps = {}

    def dump(name, ap, buf, parts=128):
        if not debug:
            return
        shp = [parts] + list(ap.shape[1:])
        dt_ = nc.dram_tensor("dbg_" + name, shp, F32, kind="ExternalOutput").ap()
        P.op("gpsimd", lambda e: e.dma_start(out=dt_, in_=ap), [buf], [], dma=Buf("dbg_" + name))

    def flat(ap):
        return ap.rearrange("p a b -> p (a b)")

    def V(fn, r=(), w=()):
        return P.op("vector", fn, r, w)

    def A(fn, r=(), w=()):
        return P.op("scalar", fn, r, w)

    def G(fn, r=(), w=()):
        return P.op("gpsimd", fn, r, w)

    def M(fn, r=(), w=(), mode=(128, 128)):
        return P.op("tensor", fn, r, w, mode=mode)

    def ld(eng, o_ap, i_ap, buf, r=()):
        return P.op(eng, lambda e: e.dma_start(out=o_ap, in_=i_ap), r, [buf], dma=buf)

    fmp, b_fmp = alloc([NFM])
    cst, b_cst = alloc([3, 128])
    identb, b_identb = alloc([128], BF16)
    w2sb, b_w2 = alloc([2, 256], BF16, parts=16)
    mhalf, b_mhalf = alloc([8])
    ones, b_ones = alloc([128])
    coef, b_coef = alloc([3, 8])
    nb2, b_nb2 = alloc([4])
    GB, b_GB = alloc([6, 8])
    Gt, b_Gt = alloc([2, D])
    ssx, b_ssx = alloc([16])
    zeros, b_zeros = alloc([64])
    persist_end = ptr[0]
    rowp, b_rowp = alloc([4096])

    ld("sync", fmp, fmp_d, b_fmp)
    ld("sync", rowp, rowp_d, b_rowp)
    ld("sync", cst, cst_d, b_cst)
    ld("gpsimd", w2sb, w2_d, b_w2)
    V(lambda e: e.tensor_copy(out=identb, in_=cst[:, 0, :]), [b_cst], [b_identb])
    G(lambda e: e.memset(mhalf, -0.5), [], [b_mhalf])
    G(lambda e: e.memset(ones, 1.0), [], [b_ones])
    G(lambda e: e.memset(zeros, 0.0), [], [b_zeros])
    maskf = cst[:, 1, :]
    maskb = cst[:, 2, :]

    tl, b_tl = alloc([8])
    lg = fmp[:, FM_LB:FM_LB + 16].rearrange("p (d s h) -> p d s h", d=2, s=2)
    V(lambda e: e.tensor_tensor(out=tl.rearrange("p (d h) -> p d h", d=2), in0=lg[:, :, 0, :], in1=lg[:, :, 1, :], op=ALU.subtract), [b_fmp], [b_tl])
    A(lambda e: e.activation(out=tl, in_=tl, func=AF.Tanh, scale=0.5), [b_tl], [b_tl])
    V(lambda e: e.tensor_scalar(out=coef[:, 0, :], in0=tl, scalar1=0.25, scalar2=0.75, op0=ALU.mult, op1=ALU.add), [b_tl], [b_coef])
    V(lambda e: e.tensor_scalar(out=coef[:, 1, :], in0=tl, scalar1=-0.25, scalar2=0.25, op0=ALU.mult, op1=ALU.add), [b_tl], [b_coef])
    V(lambda e: e.tensor_scalar(out=coef[:, 2, :], in0=tl, scalar1=0.25, scalar2=-0.25, op0=ALU.mult, op1=ALU.add), [b_tl], [b_coef])
    V(lambda e: e.tensor_scalar(out=nb2, in0=fmp[:, FM_GKB:FM_GKB + 4], scalar1=-1.0, scalar2=None, op0=ALU.mult), [b_fmp], [b_nb2])

    sc2, b_sc2 = alloc([8, 2])
    screp, b_screp = alloc([8, 128])
    tc_, b_tc = alloc([16])
    A(lambda e: e.activation(out=tc_, in_=fmp[:, 0:16], func=AF.Tanh, scale=0.5), [b_fmp], [b_tc])
    V(lambda e: e.scalar_tensor_tensor(out=tc_, in0=tc_, scalar=1.0, in1=fmp[:, 0:16], op0=ALU.add, op1=ALU.mult), [b_tc, b_fmp], [b_tc])
    V(lambda e: e.tensor_scalar(out=sc2.rearrange("p k j -> p j k"), in0=tc_.rearrange("p (j k) -> p j k", j=2), scalar1=0.5, scalar2=None, op0=ALU.mult), [b_tc], [b_sc2])
    for k in range(8):
        V(lambda e, k=k: e.tensor_scalar(out=screp[:, k, :], in0=ones, scalar1=sc2[:, k, 0:1], scalar2=None, op0=ALU.mult), [b_sc2, b_ones], [b_screp])
    stg = [alloc([8, D]) for _ in range(2)]
    modfm, b_modfm = alloc([4, 8, 2])
    aw = ada_w.rearrange("(k p) c -> p k c", p=128)
    fm_groups = {0: 0, 1: 1, 3: 2, 4: 3}
    for gi, g in enumerate([0, 1, 3, 4, 2, 5]):
        sg, bsg = stg[gi % 2]
        ld("sync", sg, aw[:, :, g * D:(g + 1) * D], bsg)
        if g in fm_groups:
            j = fm_groups[g]
            for cb in range(8):
                for k in range(8):
                    M(lambda e, sg=sg, cb=cb, k=k: e.matmul(PS[0][:, cb * 2:cb * 2 + 2], lhsT=sg[:, k, cb * 128:(cb + 1) * 128], rhs=sc2[:, k, :],
                                                              start=(cb == 0 and k == 0), stop=(k == 7), skip_group_check=True), [bsg, b_sc2], [pb[0]])
            V(lambda e, j=j: e.tensor_tensor(out=modfm[:, j, :, :], in0=PS[0][:, 0:16].rearrange("p (c j) -> p c j", j=2),
                                              in1=fmp[:, FM_ADAB + 8 * j:FM_ADAB + 8 * j + 8].unsqueeze(2).to_broadcast([128, 8, 2]), op=ALU.add), [pb[0], b_fmp], [b_modfm])
        else:
            gj = 0 if g == 2 else 1
            for hf in range(2):
                for k in range(8):
                    M(lambda e, sg=sg, hf=hf, k=k: e.matmul(PS[1 + hf][:, :], lhsT=screp[:, k, :], rhs=sg[:, k, hf * 512:(hf + 1) * 512],
                                                              start=(k == 0), stop=(k == 7)), [bsg, b_screp], [pb[1 + hf]])
                V(lambda e, gj=gj, hf=hf: e.tensor_tensor(out=Gt[:, gj, hf * 512:(hf + 1) * 512], in0=PS[1 + hf][:, :], in1=rowp[:, gj * D + hf * 512: gj * D + (hf + 1) * 512], op=ALU.add), [pb[1 + hf], b_rowp], [b_Gt])
            V(lambda e, gj=gj: e.tensor_tensor(out=Gt[:, gj, :], in0=Gt[:, gj, :], in1=rowp[:, (2 + gj) * D:(3 + gj) * D], op=ALU.mult), [b_rowp, b_Gt], [b_Gt])
    pre1 = fmp[:, FM_PRE1:FM_PRE1 + 8]
    pre2 = fmp[:, FM_PRE2:FM_PRE2 + 8]
    V(lambda e: e.scalar_tensor_tensor(out=GB[:, 0, :], in0=modfm[:, 1, :, 0], scalar=1.0, in1=pre1, op0=ALU.add, op1=ALU.mult), [b_modfm, b_fmp], [b_GB])
    V(lambda e: e.tensor_copy(out=GB[:, 1, :], in_=modfm[:, 0, :, 0]), [b_modfm], [b_GB])
    V(lambda e: e.scalar_tensor_tensor(out=GB[:, 2, :], in0=modfm[:, 1, :, 1], scalar=1.0, in1=pre1, op0=ALU.add, op1=ALU.mult), [b_modfm, b_fmp], [b_GB])
    V(lambda e: e.tensor_copy(out=GB[:, 3, :], in_=modfm[:, 0, :, 1]), [b_modfm], [b_GB])
    V(lambda e: e.scalar_tensor_tensor(out=GB[:, 4, :], in0=modfm[:, 3, :, 0], scalar=1.0, in1=pre2, op0=ALU.add, op1=ALU.mult), [b_modfm, b_fmp], [b_GB])
    V(lambda e: e.tensor_copy(out=GB[:, 5, :], in_=modfm[:, 2, :, 0]), [b_modfm], [b_GB])
    P.barrier()
    ptr[0] = persist_end

    def front(src_rows, xt, b_xt, junk, b_junk, xn, b_xn, hT, b_hT, ss, rstd, b_ss, gi, trbank, eps=EPS):
        ld("sync", xt, src_rows, b_xt)
        A(lambda e: e.activation(out=junk, in_=xt, func=AF.Square, accum_out=ss), [b_xt], [b_junk, b_ss])
        G(lambda e: e.tensor_scalar(out=ss, in0=ss, scalar1=1.0 / D, scalar2=eps, op0=ALU.mult, op1=ALU.add), [b_ss], [b_ss])
        G(lambda e: e.tensor_tensor(out=rstd, in0=ss, in1=mhalf[:, 0:1], op=ALU.pow), [b_ss, b_mhalf], [b_ss])
        V(lambda e: e.tensor_scalar(out=xn, in0=xt, scalar1=rstd, scalar2=None, op0=ALU.mult), [b_xt, b_ss], [b_xn])
        for k in range(8):
            M(lambda e, k=k: e.transpose(PSb[trbank][:, k * 128:(k + 1) * 128], xn[:, k * 128:(k + 1) * 128], identb), [b_xn, b_identb], [pb[trbank]])
        for k in range(8):
            if k % 2 == 0:
                A(lambda e, k=k: e.activation(out=hT[:, k, :], in_=PSb[trbank][:, k * 128:(k + 1) * 128], func=AF.Identity, scale=GB[:, gi, k:k + 1], bias=GB[:, gi + 1, k:k + 1]), [pb[trbank], b_GB], [b_hT])
            else:
                V(lambda e, k=k: e.tensor_scalar(out=hT[:, k, :], in0=PSb[trbank][:, k * 128:(k + 1) * 128], scalar1=GB[:, gi, k:k + 1], scalar2=GB[:, gi + 1, k:k + 1], op0=ALU.mult, op1=ALU.add), [pb[trbank], b_GB], [b_hT])

    def fm_job(bank, w, b_w, col0, nblk, hT, b_hT, m=128):
        for cb in range(nblk):
            for k in range(8):
                M(lambda e, cb=cb, k=k: e.matmul(PS[bank][0:m, cb * 128:(cb + 1) * 128], lhsT=w[:, k, col0 + cb * m: col0 + (cb + 1) * m], rhs=hT[:, k, :],
                                                 start=(cb == 0 and k == 0), stop=(k == 7), skip_group_check=True), [b_w, b_hT], [pb[bank]])

    def tm_job(bank, w, b_w, col0, hT, b_hT):
        for k in range(8):
            M(lambda e, k=k: e.matmul(PS[bank][:, :], lhsT=hT[:, k, :], rhs=w[:, k, col0:col0 + 512], start=(k == 0), stop=(k == 7)), [b_w, b_hT], [pb[bank]])

    def run_round(items):
        items = [[g_, lim, 0] for (g_, lim) in items]
        while items:
            for it_ in list(items):
                if it_[1] is not None and it_[2] >= it_[1]:
                    items.remove(it_)
                    continue
                try:
                    next(it_[0])
                    it_[2] += 1
                except StopIteration:
                    items.remove(it_)

    def scan_pass(d, nt):
        ptr[0] = persist_end
        P.pe_skip = True
        wsc, _ = alloc([8, 2576], BF16)
        wi = w_in.rearrange("(k p) c -> p k c", p=128)
        bw = []
        for (dst, src, n) in [(0, C_HQ, 512), (512, C_HF + 512 * d, 512), (1024, C_HI, 512), (1536, C_GQ, 1024), (2560, C_GR + 16 * d, 16)]:
            b_ = Buf(f"w{d}_{dst}")
            bw.append(b_)
            P.op("gpsimd", lambda e, dst=dst, src=src, n=n: e.dma_start(out=wsc[:, :, dst:dst + n], in_=wi[:, :, src:src + n]), [], [b_], dma=b_)
        b_whq, b_whf, b_whi, b_wg, b_wgr = bw
        xts = [alloc([D]) for _ in range(2)]
        hTs = [alloc([8, 128], BF16) for _ in range(2)]
        sss = [alloc([2]) for _ in range(2)]
        tmps = []
        for _ in range(2):
            t_ = {}
            t_['junk'] = alloc([D], BF16); t_['xn'] = alloc([D], BF16)
            t_['tq'] = alloc([4, 128]); t_['qq'] = alloc([4, 128])
            t_['kk'] = alloc([4, 128]); t_['ff'] = alloc([4, 128]); t_['Ei'] = alloc([4, 128]); t_['cmh'] = alloc([4, 128])
            t_['kinv'] = alloc([4, 128], BF16); t_['kend'] = alloc([4, 128], BF16)
            t_['qg'] = alloc([2, 128]); t_['kg'] = alloc([2, 128])
            t_['eg'] = alloc([2, 128]); t_['cum'] = alloc([2, 128]); t_['Egi'] = alloc([2, 128])
            t_['kgi'] = alloc([2, 128], BF16); t_['kge'] = alloc([2, 128], BF16)
            t_['rT'] = alloc([128], BF16, parts=16)
            tmps.append(t_)
        EEs = [alloc([4, 128]) for _ in range(3)]
        Egs = [alloc([2, 128]) for _ in range(3)]
        qdecs = [alloc([4, 128], BF16) for _ in range(3)]
        qgds = [alloc([2, 128], BF16) for _ in range(3)]
        vhgs = [alloc([512], BF16) for _ in range(3)]
        vgls = [alloc([512], BF16) for _ in range(3)]
        ktms = [alloc([2, 768], BF16) for _ in range(3)]
        qdcs = [[alloc([4, 128], BF16) for _ in range(2)] for _ in range(3)]
        qgcs = [[alloc([2, 128], BF16) for _ in range(2)] for _ in range(3)]
        for sl_ in range(3):
            for c_ in range(2):
                G(lambda e, a_=qdcs[sl_][c_][0]: e.memset(a_, 0.0), [], [qdcs[sl_][c_][1]])
                G(lambda e, a_=qgcs[sl_][c_][0]: e.memset(a_, 0.0), [], [qgcs[sl_][c_][1]])
        attms = [alloc([4, 128], BF16) for _ in range(3)]
        attgs = [alloc([4, 128], BF16) for _ in range(3)]
        osb, b_osb = alloc([D]); obt, b_obt = alloc([D])
        maskc, b_maskc = alloc([512])
        G(lambda e: e.memset(maskc, 1.0), [], [b_maskc])
        zc = 0 if d == 0 else 63
        G(lambda e: e.memset(maskc.rearrange("p (c j) -> p c j", j=64)[:, :, zc:zc + 1], 0.0), [], [b_maskc])
        S_hg, b_Shg = alloc([4, 128]); S_hgb, b_Shgb = alloc([4, 128], BF16)
        S_gl, b_Sgl = alloc([2, 256]); S_glb, b_Sglb = alloc([2, 256], BF16)
        mask = maskf if d == 0 else maskb
        corder = [0, 1] if d == 0 else [1, 0]
        endcol = (lambda c: c * 64 + 63) if d == 0 else (lambda c: c * 64)
        G(lambda e: e.memset(S_hg, 0.0), [], [b_Shg]); G(lambda e: e.memset(S_hgb, 0.0), [], [b_Shgb])
        G(lambda e: e.memset(S_gl, 0.0), [], [b_Sgl]); G(lambda e: e.memset(S_glb, 0.0), [], [b_Sglb])

        seq = [("c", i) for i in range(2)] + [("x", i) for i in range(nt)]
        if d == 1:
            seq = [("c", 1), ("c", 0)] + [("x", i) for i in range(NT - 1, NT - 1 - nt, -1)]

        def stage1(idx):
            kind, i = seq[idx]
            compute_o = kind == "x"
            s = idx % 2
            s3 = idx % 3
            hT, b_hT = hTs[s]
            EE, b_EE = EEs[s3]; Eg, b_Eg = Egs[s3]
            qdec, b_qdec = qdecs[s3]; qgd, b_qgd = qgds[s3]
            vhg, b_vhg = vhgs[s3]; vgl, b_vgl = vgls[s3]
            ktm, b_ktm = ktms[s3]
            attm, b_attm = attms[s3]; attg, b_attg = attgs[s3]
            t_ = tmps[s]
            junk, b_junk = t_['junk']; xn, b_xn = t_['xn']; tq, b_tq = t_['tq']; qq, b_qq = t_['qq']
            cmh, b_cmh = t_['cmh']
            kk, b_kk = t_['kk']; ff, b_ff = t_['ff']; Ei, b_Ei = t_['Ei']; kinv, b_kinv = t_['kinv']; kend, b_kend = t_['kend']
            qg, b_qg = t_['qg']; kg, b_kg = t_['kg']; eg, b_eg = t_['eg']; cum, b_cum = t_['cum']; Egi, b_Egi = t_['Egi']
            kgi, b_kgi = t_['kgi']; kge, b_kge = t_['kge']; rT, b_rT = t_['rT']
            rows = (ctx if kind == "c" else x)[i * TT:(i + 1) * TT, :]
            front(rows, xts[s][0], xts[s][1], junk, b_junk, xn, b_xn, hT, b_hT, sss[s][0][:, 0:1], sss[s][0][:, 1:2], sss[s][1], 2 if kind == "c" else 0, 3)
            yield
            fm_job(1, wsc, b_whf, 512, 4, hT, b_hT)
            A(lambda e: e.activation(out=flat(kk), in_=PS[1][:, :], func=AF.Tanh, scale=0.5), [pb[1]], [b_kk])
            yield
            srows = slice(i * TT, (i + 1) * TT)
            use_load = (d == 0 and kind == "x")
            do_store = (d == 1 and kind == "x")
            if use_load:
                ld("sync", flat(qq), sq_q[srows, :], b_qq)
                ld("sync", flat(qg), sq_g[srows, 0:256], b_qg)
                ld("sync", flat(kg), sq_g[srows, 256:512], b_kg)
                ld("sync", vhg, sq_v[srows, 0:512], b_vhg)
                ld("sync", vgl, sq_v[srows, 512:1024], b_vgl)
            else:
                fm_job(2, wsc, b_whq, 0, 4, hT, b_hT)
                A(lambda e: e.activation(out=flat(tq), in_=PS[2][:, :], func=AF.Tanh, scale=0.5), [pb[2]], [b_tq])
                V(lambda e: e.scalar_tensor_tensor(out=flat(qq), in0=flat(tq), scalar=1.0, in1=PS[2][:, :], op0=ALU.add, op1=ALU.mult), [b_tq, pb[2]], [b_qq])
                if do_store:
                    ld("sync", sq_q[srows, :], flat(qq), b_qq, [b_qq])
            for h in range(4):
                ci = d * 4 + h
                G(lambda e, h=h, ci=ci: e.tensor_scalar(out=ff[:, h, :], in0=kk[:, h, :], scalar1=coef[:, 1, ci:ci + 1], scalar2=coef[:, 0, ci:ci + 1], op0=ALU.mult, op1=ALU.add), [b_kk, b_coef], [b_ff])
            A(lambda e: e.activation(out=ff, in_=ff, func=AF.Ln), [b_ff], [b_ff])
            for h in range(4):
                ci = d * 4 + h
                G(lambda e, h=h, ci=ci: e.tensor_scalar(out=kk[:, h, :], in0=kk[:, h, :], scalar1=coef[:, 2, ci:ci + 1], scalar2=coef[:, 1, ci:ci + 1], op0=ALU.mult, op1=ALU.add), [b_kk, b_coef], [b_kk])
            yield
            if not use_load:
                fm_job(1, wsc, b_wg, 1536, 4, hT, b_hT)
                A(lambda e: e.activation(out=flat(qg), in_=PS[1][:, 0:256], func=AF.Copy), [pb[1]], [b_qg])
                A(lambda e: e.activation(out=flat(kg), in_=PS[1][:, 256:512], func=AF.Copy), [pb[1]], [b_kg])
                if do_store:
                    ld("sync", sq_g[srows, 0:256], flat(qg), b_qg, [b_qg])
                    ld("sync", sq_g[srows, 256:512], flat(kg), b_kg, [b_kg])
            if d == 0:
                V(lambda e: e.tensor_tensor_scan(out=flat(cmh), data0=maskc, data1=flat(ff), initial=0.0, op0=ALU.mult, op1=ALU.add), [b_ff, b_maskc], [b_cmh])
            else:
                V(lambda e: e.tensor_tensor_scan(out=flat(cmh)[:, ::-1], data0=maskc[:, ::-1], data1=flat(ff)[:, ::-1], initial=0.0, op0=ALU.mult, op1=ALU.add), [b_ff, b_maskc], [b_cmh])
            A(lambda e: e.activation(out=EE, in_=cmh, func=AF.Exp), [b_cmh], [b_EE])
            A(lambda e: e.activation(out=Ei, in_=cmh, func=AF.Exp, scale=-1.0), [b_cmh], [b_Ei])
            yield
            for k in range(8):
                M(lambda e, k=k: e.matmul(PS[2][0:16, 0:128], lhsT=wsc[:, k, 2560:2576], rhs=hT[:, k, :], start=(k == 0), stop=(k == 7)), [b_wgr, b_hT], [pb[2]], mode=(128, 32))
            A(lambda e: e.activation(out=rT, in_=PS[2][0:16, 0:128], func=AF.Copy), [pb[2]], [b_rT])
            for t in range(2):
                M(lambda e, t=t: e.matmul(PS[2][:, 128 + t * 128:256 + t * 128], lhsT=w2sb[:, d, t * 128:(t + 1) * 128], rhs=rT, start=(t == 0), stop=True, skip_group_check=True), [b_w2, b_rT], [pb[2]], mode=(32, 128))
            for t in range(2):
                A(lambda e, t=t: e.activation(out=eg[:, t, :], in_=PS[2][:, 128 + t * 128:256 + t * 128], func=AF.Exp, scale=-1.0, bias=nb2[:, d * 2 + t:d * 2 + t + 1]), [pb[2], b_nb2], [b_eg])
            A(lambda e: e.activation(out=eg, in_=eg, func=AF.Ln, bias=1.0), [b_eg], [b_eg])
            V(lambda e: e.scalar_tensor_tensor(out=qdec, in0=qq, scalar=0.5 * 128 ** -0.5, in1=EE, op0=ALU.mult, op1=ALU.mult), [b_qq, b_EE], [b_qdec])
            for c in range(2):
                sl = slice(c * 64, (c + 1) * 64)
                qa, qb_ = qdcs[s3][c]
                V(lambda e, sl=sl, qa=qa: e.scalar_tensor_tensor(out=qa[:, :, sl], in0=qq[:, :, sl], scalar=0.5 * 128 ** -0.5, in1=EE[:, :, sl], op0=ALU.mult, op1=ALU.mult), [b_qq, b_EE], [qb_])
            V(lambda e: e.tensor_tensor(out=kinv, in0=kk, in1=Ei, op=ALU.mult), [b_kk, b_Ei], [b_kinv])
            ec0 = endcol(0)
            V(lambda e: e.tensor_tensor(out=kend.rearrange("p h (c j) -> p h c j", j=64), in0=kinv.rearrange("p h (c j) -> p h c j", j=64),
                                        in1=EE.rearrange("p h (c j) -> p h c j", j=64)[:, :, :, ec0:ec0 + 1].to_broadcast([128, 4, 2, 64]), op=ALU.mult), [b_kinv, b_EE], [b_kend])
            yield
            if not use_load:
                tm_job(1, wsc, b_whi, 1024, hT, b_hT)
                A(lambda e: e.activation(out=vhg, in_=PS[1][:, :], func=AF.Copy), [pb[1]], [b_vhg])
                if do_store:
                    ld("sync", sq_v[srows, 0:512], vhg, b_vhg, [b_vhg])
            if d == 0:
                V(lambda e: e.tensor_tensor_scan(out=flat(cum), data0=maskc[:, 0:256], data1=flat(eg), initial=0.0, op0=ALU.mult, op1=ALU.add), [b_eg, b_maskc], [b_cum])
            else:
                V(lambda e: e.tensor_tensor_scan(out=flat(cum)[:, ::-1], data0=maskc[:, 0:256][:, ::-1], data1=flat(eg)[:, ::-1], initial=0.0, op0=ALU.mult, op1=ALU.add), [b_eg, b_maskc], [b_cum])
            A(lambda e: e.activation(out=Eg, in_=cum, func=AF.Exp, scale=-1.0 / 16), [b_cum], [b_Eg])
            A(lambda e: e.activation(out=Egi, in_=cum, func=AF.Exp, scale=1.0 / 16), [b_cum], [b_Egi])
            yield
            if not use_load:
                tm_job(2, wsc, b_wg, 2048, hT, b_hT)
                A(lambda e: e.activation(out=vgl, in_=PS[2][:, :], func=AF.Copy), [pb[2]], [b_vgl])
                if do_store:
                    ld("sync", sq_v[srows, 512:1024], vgl, b_vgl, [b_vgl])
            V(lambda e: e.scalar_tensor_tensor(out=qgd, in0=qg, scalar=0.125, in1=Eg, op0=ALU.mult, op1=ALU.mult), [b_qg, b_Eg], [b_qgd])
            for c in range(2):
                sl = slice(c * 64, (c + 1) * 64)
                qa, qb_ = qgcs[s3][c]
                V(lambda e, sl=sl, qa=qa: e.scalar_tensor_tensor(out=qa[:, :, sl], in0=qg[:, :, sl], scalar=0.125, in1=Eg[:, :, sl], op0=ALU.mult, op1=ALU.mult), [b_qg, b_Eg], [qb_])
            V(lambda e: e.tensor_tensor(out=kgi, in0=kg, in1=Egi, op=ALU.mult), [b_kg, b_Egi], [b_kgi])
            V(lambda e: e.tensor_tensor(out=kge.rearrange("p h (c j) -> p h c j", j=64), in0=kgi.rearrange("p h (c j) -> p h c j", j=64),
                                        in1=Eg.rearrange("p h (c j) -> p h c j", j=64)[:, :, :, ec0:ec0 + 1].to_broadcast([128, 2, 2, 64]), op=ALU.mult), [b_kgi, b_Eg], [b_kge])
            yield
            for h in range(4):
                M(lambda e, h=h: e.transpose(PSb[3][:, h * 128:(h + 1) * 128], kend[:, h, :], identb), [b_kend, b_identb], [pb[3]])
            for t in range(2):
                M(lambda e, t=t: e.transpose(PSb[3][:, 512 + t * 128:640 + t * 128], kge[:, t, :], identb), [b_kge, b_identb], [pb[3]])
            A(lambda e: e.activation(out=ktm[:, 0, :], in_=PSb[3][:, 0:768], func=AF.Copy, scale=maskb[:, 0:1]), [pb[3], b_cst], [b_ktm])
            A(lambda e: e.activation(out=ktm[:, 1, :], in_=PSb[3][:, 0:768], func=AF.Copy, scale=maskf[:, 127:128]), [pb[3], b_cst], [b_ktm])
            if compute_o:
                for h in range(4):
                    M(lambda e, h=h: e.matmul(PS[4][:, h * 128:(h + 1) * 128], lhsT=kinv[:, h, :], rhs=qdec[:, h, :], start=(h == 0), stop=True, skip_group_check=True), [b_kinv, b_qdec], [pb[4]])
                V(lambda e: e.tensor_tensor(out=attm, in0=PS[4][:, :].rearrange("p (h i) -> p h i", h=4), in1=mask.unsqueeze(1).to_broadcast([128, 4, 128]), op=ALU.mult), [pb[4], b_cst], [b_attm])
                yield
                for h in (0, 2, 1, 3):
                    p0 = (h % 2) * 64
                    t = h // 2
                    M(lambda e, h=h, p0=p0, t=t: e.matmul(PS[4][:, h * 128:(h + 1) * 128], lhsT=kgi[p0:p0 + 64, t, :], rhs=qgd[p0:p0 + 64, t, :], start=(h == 0), stop=True, skip_group_check=True), [b_kgi, b_qgd], [pb[4]], mode=(64, 128, p0))
                V(lambda e: e.tensor_tensor(out=attg, in0=PS[4][:, :].rearrange("p (h i) -> p h i", h=4), in1=mask.unsqueeze(1).to_broadcast([128, 4, 128]), op=ALU.mult), [pb[4], b_cst], [b_attg])
            yield

        def stage2(idx):
            kind, i = seq[idx]
            compute_o = kind == "x"
            s = idx % 3
            EE, b_EE = EEs[s]; Eg, b_Eg = Egs[s]
            qdec, b_qdec = qdecs[s]; qgd, b_qgd = qgds[s]
            vhg, b_vhg = vhgs[s]; vgl, b_vgl = vgls[s]
            ktm, b_ktm = ktms[s]
            attm, b_attm = attms[s]; attg, b_attg = attgs[s]
            if compute_o:
                for h in range(4):
                    M(lambda e, h=h: e.matmul(PS[7][:, h * 128:(h + 1) * 128], lhsT=attm[:, h, :], rhs=vhg[:, h * 128:(h + 1) * 128], start=(h == 0), stop=False, skip_group_check=True), [b_attm, b_vhg], [pb[7]])
                for h in range(4):
                    M(lambda e, h=h: e.matmul(PS[0][:, h * 128:(h + 1) * 128], lhsT=attg[:, h, :], rhs=vgl[:, h * 128:(h + 1) * 128], start=(h == 0), stop=False, skip_group_check=True), [b_attg, b_vgl], [pb[0]])
                yield
            for c in corder:
                ts = slice(c * 64, (c + 1) * 64)
                ec = endcol(c)
                if compute_o:
                    qa, qb_ = qdcs[s][c]
                    ga_, gb_ = qgcs[s][c]
                    for h in range(4):
                        M(lambda e, h=h, qa=qa: e.matmul(PS[7][:, h * 128:(h + 1) * 128], lhsT=qa[:, h, :], rhs=S_hgb[:, h, :], start=False, stop=True, skip_group_check=True), [qb_, b_Shgb], [pb[7]])
                    for t in range(2):
                        M(lambda e, t=t, ga_=ga_: e.matmul(PS[0][:, t * 256:(t + 1) * 256], lhsT=ga_[:, t, :], rhs=S_glb[:, t, :], start=False, stop=True, skip_group_check=True), [gb_, b_Sglb], [pb[0]])
                for h in range(4):
                    M(lambda e, h=h, c=c: e.matmul(PS[5][:, h * 128:(h + 1) * 128], lhsT=ktm[:, c, h * 128:(h + 1) * 128], rhs=vhg[:, h * 128:(h + 1) * 128], start=(h == 0), stop=True, skip_group_check=True), [b_ktm, b_vhg], [pb[5]])
                for t in range(2):
                    M(lambda e, t=t, c=c: e.matmul(PS[6][:, t * 256:(t + 1) * 256], lhsT=ktm[:, c, 512 + t * 128:640 + t * 128], rhs=vgl[:, t * 256:(t + 1) * 256], start=(t == 0), stop=True, skip_group_check=True), [b_ktm, b_vgl], [pb[6]])
                yield
                for h in range(4):
                    V(lambda e, h=h, ec=ec: e.scalar_tensor_tensor(out=S_hg[:, h, :], in0=S_hg[:, h, :], scalar=EE[:, h, ec:ec + 1], in1=PS[5][:, h * 128:(h + 1) * 128], op0=ALU.mult, op1=ALU.add), [b_Shg, b_EE, pb[5]], [b_Shg])
                G(lambda e: e.tensor_copy(out=S_hgb, in_=S_hg), [b_Shg], [b_Shgb])
                for t in range(2):
                    for hh in range(2):
                        ps_ = slice(hh * 64, (hh + 1) * 64)
                        cs = slice(hh * 128, (hh + 1) * 128)
                        V(lambda e, t=t, ps_=ps_, cs=cs, ec=ec: e.scalar_tensor_tensor(out=S_gl[ps_, t, cs], in0=S_gl[ps_, t, cs], scalar=Eg[ps_, t, ec:ec + 1], in1=PS[6][ps_, t * 256 + cs.start:t * 256 + cs.stop], op0=ALU.mult, op1=ALU.add), [b_Sgl, b_Eg, pb[6]], [b_Sgl])
                G(lambda e: e.tensor_copy(out=S_glb, in_=S_gl), [b_Sgl], [b_Sglb])
                yield
            if compute_o:
                rows = slice(i * TT, (i + 1) * TT)
                if d == 1:
                    A(lambda e: e.activation(out=osb[:, 0:512], in_=PS[7][:, :], func=AF.Copy), [pb[7]], [b_osb])
                    V(lambda e: e.tensor_copy(out=osb[:, 512:1024], in_=PS[0][:, :]), [pb[0]], [b_osb])
                    ld("sync", obw[rows, :], osb, b_osb, [b_osb])
                else:
                    ld("sync", obt, obw[rows, :], b_obt)
                    V(lambda e: e.tensor_tensor(out=osb[:, 0:512], in0=PS[7][:, :], in1=obt[:, 0:512], op=ALU.add), [pb[7], b_obt], [b_osb])
                    V(lambda e: e.tensor_tensor(out=osb[:, 512:1024], in0=PS[0][:, :], in1=obt[:, 512:1024], op=ALU.add), [pb[0], b_obt], [b_osb])
                    ld("sync", osum[rows, :], osb, b_osb, [b_osb])
            yield

        def interleave(*gens):
            gens = list(gens)
            while gens:
                for g_ in list(gens):
                    try:
                        next(g_)
                    except StopIteration:
                        gens.remove(g_)

        n_ = len(seq)
        H1 = 4
        g1 = {}
        for r in range(n_ + 2):
            items = []
            if 0 <= r - 2 < n_:
                items.append((stage2(r - 2), None))
            if 0 <= r - 1 < n_:
                items.append((g1.pop(r - 1), None))
            if r < n_:
                g1[r] = stage1(r)
                items.append((g1[r], H1))
            run_round(items)
        P.barrier()

    def interleave(*gens):
        gens = list(gens)
        while gens:
            for g_ in list(gens):
                try:
                    next(g_)
                except StopIteration:
                    gens.remove(g_)

    def out_pass(nt):
        ptr[0] = persist_end
        P.pe_skip = True
        wg, _ = alloc([8, 3072], BF16)
        wi = w_in.rearrange("(k p) c -> p k c", p=128)
        bwg = []
        for (dst, src, n) in [(0, C_HOG, 512), (512, C_GOG, 512), (1024, C_GA, 1024), (2048, C_GB, 1024)]:
            b_ = Buf(f"wg{dst}")
            bwg.append(b_)
            P.op("gpsimd", lambda e, dst=dst, src=src, n=n: e.dma_start(out=wg[:, :, dst:dst + n], in_=wi[:, :, src:src + n]), [], [b_], dma=b_)
        wab, b_wab = alloc([8, D], BF16)
        wo, b_wo = alloc([8, D], BF16)
        P.op("gpsimd", lambda e: e.dma_start(out=wo, in_=w_o.rearrange("(k p) c -> p k c", p=128)), [], [b_wo], dma=Buf("wo"))
        wst_off = ptr[0]
        wst, b_wst = alloc([4, D])
        for j, wsrc in enumerate([w_a, w_b]):
            ld("sync", wst, wsrc.rearrange("(h p) c -> p h c", p=128), b_wst)
            V(lambda e, j=j: e.tensor_scalar(out=wab[:, 4 * j:4 * j + 4, :], in0=wst, scalar1=fmp[:, FM_ON + j:FM_ON + j + 1], scalar2=None, op0=ALU.mult), [b_wst, b_fmp], [b_wab])
        P.barrier()
        ptr[0] = wst_off
        xts = [alloc([D]) for _ in range(3)]
        hTs = [alloc([8, 128], BF16) for _ in range(2)]
        sss = [alloc([2]) for _ in range(2)]
        junk, b_junk = alloc([D], BF16)
        xn, b_xn = alloc([D], BF16)
        tg, b_tg = alloc([D])
        sogs = [alloc([D]) for _ in range(3)]
        thas = [alloc([D], BF16) for _ in range(3)]
        thbs = [alloc([D], BF16) for _ in range(3)]
        tB = []
        for _ in range(2):
            t_ = {}
            t_['os'] = alloc([D]); t_['sq'] = alloc([D]); t_['ssq'] = alloc([16])
            t_['yy'] = alloc([D], BF16); t_['yT'] = alloc([8, 128], BF16)
            t_['m1'] = alloc([D]); t_['m2'] = alloc([D]); t_['mT'] = alloc([8, 128], BF16)
            t_['mtm'] = alloc([D], BF16); t_['x1'] = alloc([D]); t_['ssx'] = alloc([4])
            tB.append(t_)
        print("out_pass sbuf words", ptr[0])

        def stageA(i):
            s = i % 2
            s3 = i % 3
            hT, b_hT = hTs[s]
            sog, b_sog = sogs[s3]; tha, b_tha = thas[s3]; thb, b_thb = thbs[s3]
            front(x[i * TT:(i + 1) * TT, :], xts[s3][0], xts[s3][1], junk, b_junk, xn, b_xn, hT, b_hT, sss[s][0][:, 0:1], sss[s][0][:, 1:2], sss[s][1], 0, 0)
            yield
            for j in range(2):
                bk = 1 + j
                tm_job(bk, wg, bwg[j], 512 * j, hT, b_hT)
                A(lambda e, j=j, bk=bk: e.activation(out=tg[:, j * 512:(j + 1) * 512], in_=PS[bk][:, :], func=AF.Tanh, scale=0.5), [pb[bk]], [b_tg])
                V(lambda e, j=j, bk=bk: e.scalar_tensor_tensor(out=sog[:, j * 512:(j + 1) * 512], in0=tg[:, j * 512:(j + 1) * 512], scalar=1.0, in1=PS[bk][:, :], op0=ALU.add, op1=ALU.mult), [b_tg, pb[bk]], [b_sog])
            yield
            for j in range(4):
                bk = 1 + j % 2
                tm_job(bk, wg, bwg[2 + j // 2], 1024 + 512 * j, hT, b_hT)
                dst = (tha if j < 2 else thb)[:, (j % 2) * 512:(j % 2 + 1) * 512]
                bd = b_tha if j < 2 else b_thb
                A(lambda e, dst=dst, bk=bk: e.activation(out=dst, in_=PS[bk][:, :], func=AF.Tanh, scale=0.5), [pb[bk]], [bd])
                if j % 2 == 1:
                    yield

        def stageB(i):
            s = i % 3
            xt, b_xt = xts[s]
            sog, b_sog = sogs[s]; tha, b_tha = thas[s]; thb, b_thb = thbs[s]
            t_ = tB[i % 2]
            os_, b_os = t_['os']; sq, b_sq = t_['sq']; ssq, b_ssq = t_['ssq']; yy, b_yy = t_['yy']; yT, b_yT = t_['yT']
            m1, b_m1 = t_['m1']; m2, b_m2 = t_['m2']; mT, b_mT = t_['mT']; mtm, b_mtm = t_['mtm']; x1, b_x1 = t_['x1']
            ssx, b_ssx = t_['ssx']
            rows = slice(i * TT, (i + 1) * TT)
            ld("sync", os_, osum[rows, :], b_os)
            A(lambda e: e.activation(out=sq, in_=os_, func=AF.Square), [b_os], [b_sq])
            V(lambda e: e.tensor_reduce(out=ssq[:, 0:8], in_=sq.rearrange("p (h v) -> p h v", h=8), axis=AX.X, op=ALU.add), [b_sq], [b_ssq])
            G(lambda e: e.tensor_scalar(out=ssq[:, 0:8], in0=ssq[:, 0:8], scalar1=1.0 / 128, scalar2=EPS, op0=ALU.mult, op1=ALU.add), [b_ssq], [b_ssq])
            G(lambda e: e.tensor_tensor(out=ssq[:, 8:16], in0=ssq[:, 0:8], in1=mhalf, op=ALU.pow), [b_ssq, b_mhalf], [b_ssq])
            yield
            for h in range(8):
                V(lambda e, h=h: e.scalar_tensor_tensor(out=yy[:, h * 128:(h + 1) * 128], in0=os_[:, h * 128:(h + 1) * 128], scalar=ssq[:, 8 + h:9 + h], in1=sog[:, h * 128:(h + 1) * 128], op0=ALU.mult, op1=ALU.mult), [b_os, b_ssq, b_sog], [b_yy])
            for h in range(8):
                M(lambda e, h=h: e.transpose(PSb[3][:, h * 128:(h + 1) * 128], yy[:, h * 128:(h + 1) * 128], identb), [b_yy, b_identb], [pb[3]])
            A(lambda e: e.activation(out=yT, in_=PSb[3][:, :].rearrange("p (h t) -> p h t", h=8), func=AF.Copy), [pb[3]], [b_yT])
            yield
            for j, (bk0, mm, bm, th, bth) in enumerate([(4, m1, b_m1, tha, b_tha), (6, m2, b_m2, thb, b_thb)]):
                for hf in range(2):
                    bank = bk0 + hf
                    for kc in range(4):
                        M(lambda e, bank=bank, hf=hf, kc=kc, j=j: e.matmul(PS[bank][:, :], lhsT=yT[:, 4 * j + kc, :], rhs=wab[:, 4 * j + kc, hf * 512:(hf + 1) * 512], start=(kc == 0), stop=(kc == 3)), [b_wab, b_yT], [pb[bank]])
                    V(lambda e, hf=hf, bank=bank, mm=mm, th=th: e.scalar_tensor_tensor(out=mm[:, hf * 512:(hf + 1) * 512], in0=th[:, hf * 512:(hf + 1) * 512], scalar=1.0, in1=PS[bank][:, :], op0=ALU.add, op1=ALU.mult), [bth, pb[bank]], [bm])
                yield
            G(lambda e: e.tensor_tensor(out=mtm, in0=m1, in1=m2, op=ALU.add), [b_m1, b_m2], [b_mtm])
            for k in range(8):
                M(lambda e, k=k: e.transpose(PSb[3][:, k * 128:(k + 1) * 128], mtm[:, k * 128:(k + 1) * 128], identb), [b_mtm, b_identb], [pb[3]])
            A(lambda e: e.activation(out=mT, in_=PSb[3][:, :].rearrange("p (h t) -> p h t", h=8), func=AF.Copy), [pb[3]], [b_mT])
            yield
            for hf in range(2):
                for k in range(8):
                    M(lambda e, hf=hf, k=k: e.matmul(PS[4 + hf][:, :], lhsT=mT[:, k, :], rhs=wo[:, k, hf * 512:(hf + 1) * 512], start=(k == 0), stop=(k == 7)), [b_mT, b_wo], [pb[4 + hf]])
                A(lambda e, hf=hf: e.activation(out=sq[:, hf * 512:(hf + 1) * 512], in_=PS[4 + hf][:, :], func=AF.Square, accum_out=ssx[:, hf:hf + 1]), [pb[4 + hf]], [b_sq, b_ssx])
            G(lambda e: e.tensor_tensor(out=ssx[:, 2:3], in0=ssx[:, 0:1], in1=ssx[:, 1:2], op=ALU.add), [b_ssx], [b_ssx])
            G(lambda e: e.tensor_scalar(out=ssx[:, 2:3], in0=ssx[:, 2:3], scalar1=1.0 / D, scalar2=16 * EPS, op0=ALU.mult, op1=ALU.add), [b_ssx], [b_ssx])
            G(lambda e: e.tensor_tensor(out=ssx[:, 3:4], in0=ssx[:, 2:3], in1=mhalf[:, 0:1], op=ALU.pow), [b_ssx, b_mhalf], [b_ssx])
            for hf in range(2):
                V(lambda e, hf=hf: e.scalar_tensor_tensor(out=x1[:, hf * 512:(hf + 1) * 512], in0=PS[4 + hf][:, :], scalar=ssx[:, 3:4], in1=Gt[:, 0, hf * 512:(hf + 1) * 512], op0=ALU.mult, op1=ALU.mult), [pb[4 + hf], b_ssx, b_Gt], [b_x1])
            G(lambda e, xt=xt: e.tensor_tensor(out=x1, in0=x1, in1=xt, op=ALU.add), [b_x1, b_xt], [b_x1])
            ld("sync", x1s[rows, :], x1, b_x1, [b_x1])
            yield

        HB = 3
        gB = {}
        for r in range(nt + 2):
            items = []
            if 0 <= r - 2 < nt:
                items.append((gB.pop(r - 2), None))
            if 0 <= r - 1 < nt:
                gB[r - 1] = stageB(r - 1)
                items.append((gB[r - 1], HB))
            if r < nt:
                items.append((stageA(r), None))
            run_round(items)
        P.barrier()

    def ffn_pass(nt):
        ptr[0] = persist_end
        P.pe_skip = True
        wup, b_wup = alloc([8, 2 * FH], BF16)
        bwu = []
        for j in range(4):
            b_ = Buf(f"wup{j}")
            bwu.append(b_)
            P.op("gpsimd", lambda e, j=j: e.dma_start(out=wup[:, :, j * 1408:(j + 1) * 1408], in_=w_up.rearrange("(k p) c -> p k c", p=128)[:, :, j * 1408:(j + 1) * 1408]), [], [b_], dma=b_)
        wdn, _ = alloc([NFB, D], BF16)
        bwd = []
        for j in range(2):
            b_ = Buf(f"wdn{j}")
            bwd.append(b_)
            P.op("gpsimd", lambda e, j=j: e.dma_start(out=wdn[:, j * 11:(j + 1) * 11, :], in_=w_dn.rearrange("(f p) c -> p f c", p=128)[:, j * 11:(j + 1) * 11, :]), [], [b_], dma=b_)
        xf, b_xf = alloc([D])
        res, b_res = xf, b_xf
        junk, b_junk = alloc([D], BF16)
        xn, b_xn = alloc([D], BF16)
        h2Ts = [alloc([8, 640], BF16) for _ in range(2)]
        ssc, b_ssc = alloc([2])
        usbs = [alloc([10, 66], BF16) for _ in range(2)]
        dgs = [alloc([9, 128], BF16) for _ in range(2)]
        gl, b_gl = alloc([512], BF16)
        aT, b_aT = alloc([NFB, 512], BF16)
        tmp, b_tmp = alloc([512])
        cw = fmp[:, FM_CW:FM_CW + 198].rearrange("p (f t) -> p f t", t=9)
        for u_ in usbs:
            G(lambda e, u_=u_: e.memset(u_[0], 0.0), [], [u_[1]])
        last = [None]

        def wub(col):
            return bwu[col // 1408]

        def frontgen(it):
            t0 = it * 512
            h2T, b_h2T = h2Ts[it % 2]
            for s in range(5):
                c0 = s * 128
                if s < 4:
                    ld("sync", xf, x1s[t0 + s * 128:t0 + (s + 1) * 128, :], b_xf)
                else:
                    G(lambda e: e.memset(xf, 0.0), [], [b_xf])
                    if it > 0:
                        ld("sync", xf[0:64, :], x1s[t0 - 64:t0, :], b_xf)
                    if it < nt - 1:
                        ld("sync", xf[64:128, :], x1s[t0 + 512:t0 + 576, :], b_xf)
                A(lambda e: e.activation(out=junk, in_=xf, func=AF.Square, accum_out=ssc[:, 0:1]), [b_xf], [b_junk, b_ssc])
                G(lambda e: e.tensor_scalar(out=ssc[:, 0:1], in0=ssc[:, 0:1], scalar1=1.0 / D, scalar2=EPS, op0=ALU.mult, op1=ALU.add), [b_ssc], [b_ssc])
                G(lambda e: e.tensor_tensor(out=ssc[:, 1:2], in0=ssc[:, 0:1], in1=mhalf[:, 0:1], op=ALU.pow), [b_ssc, b_mhalf], [b_ssc])
                A(lambda e: e.activation(out=xn, in_=xf, func=AF.Copy, scale=ssc[:, 1:2]), [b_xf, b_ssc], [b_xn])
                for k in range(8):
                    M(lambda e, k=k: e.transpose(PSb[0][:, k * 128:(k + 1) * 128], xn[:, k * 128:(k + 1) * 128], identb), [b_xn, b_identb], [pb[0]])
                for k in range(8):
                    if k % 2 == 0:
                        A(lambda e, k=k, c0=c0: e.activation(out=h2T[:, k, c0:c0 + 128], in_=PSb[0][:, k * 128:(k + 1) * 128], func=AF.Identity, scale=GB[:, 4, k:k + 1], bias=GB[:, 5, k:k + 1]), [pb[0], b_GB], [b_h2T])
                    else:
                        V(lambda e, k=k, c0=c0: e.tensor_scalar(out=h2T[:, k, c0:c0 + 128], in0=PSb[0][:, k * 128:(k + 1) * 128], scalar1=GB[:, 4, k:k + 1], scalar2=GB[:, 5, k:k + 1], op0=ALU.mult, op1=ALU.add), [pb[0], b_GB], [b_h2T])
                yield
            if it == 0:
                G(lambda e: e.memset(h2T[:, :, 512:576], 0.0), [], [b_h2T])
            if it == nt - 1:
                G(lambda e: e.memset(h2T[:, :, 576:640], 0.0), [], [b_h2T])
            yield

        def body(it):
            t0 = it * 512
            h2T, b_h2T = h2Ts[it % 2]

            def conv(fb):
                usb, b_usb = usbs[fb % 2]
                dg, b_dg = dgs[fb % 2]
                vb = 3 + (fb % 2)
                pe_taps = [0, 1, 2, 6, 7, 8]
                for tap in pe_taps:
                    dr, dc = tap // 3 - 1, tap % 3 - 1
                    M(lambda e, tap=tap, dr=dr, dc=dc: e.matmul(PS[6][:, :], lhsT=dg[:, tap, :], rhs=usb[:, 1 + dr:9 + dr, 1 + dc:65 + dc], start=(tap == pe_taps[0]), stop=(tap == pe_taps[-1])), [b_dg, b_usb], [pb[6]])
                acc3 = tmp.rearrange("p (r c) -> p r c", r=8)
                V(lambda e: e.tensor_scalar(out=acc3, in0=usb[:, 1:9, 0:64], scalar1=cw[:, fb, 3:4], scalar2=None, op0=ALU.mult), [b_usb, b_fmp], [b_tmp])
                V(lambda e: e.scalar_tensor_tensor(out=acc3, in0=usb[:, 1:9, 1:65], scalar=cw[:, fb, 4:5], in1=acc3, op0=ALU.mult, op1=ALU.add), [b_usb, b_fmp, b_tmp], [b_tmp])
                V(lambda e: e.scalar_tensor_tensor(out=acc3, in0=usb[:, 1:9, 2:66], scalar=cw[:, fb, 5:6], in1=acc3, op0=ALU.mult, op1=ALU.add), [b_usb, b_fmp, b_tmp], [b_tmp])
                V(lambda e: e.tensor_tensor(out=tmp, in0=PS[6][:, :], in1=tmp, op=ALU.add), [pb[6], b_tmp], [b_tmp])
                A(lambda e: e.activation(out=gl, in_=tmp, func=AF.Gelu_apprx_tanh, bias=fmp[:, FM_CB + fb:FM_CB + fb + 1]), [b_tmp, b_fmp], [b_gl])
                V(lambda e: e.tensor_tensor(out=aT[:, fb, :], in0=gl, in1=PS[vb][:, :], op=ALU.mult), [b_gl, pb[vb]], [b_aT])

            for fb in range(NFB):
                ub, vb = 1 + (fb % 2), 3 + (fb % 2)
                usb, b_usb = usbs[fb % 2]
                dg, b_dg = dgs[fb % 2]
                bu = wub(fb * 128)
                bv = wub(FH + fb * 128)
                for k in range(8):
                    M(lambda e, k=k, ub=ub, fb=fb: e.matmul(PS[ub][:, :], lhsT=wup[:, k, fb * 128:(fb + 1) * 128], rhs=h2T[:, k, 0:512], start=(k == 0), stop=(k == 7)), [bu, b_h2T], [pb[ub]])
                for k in range(8):
                    M(lambda e, k=k, fb=fb: e.matmul(PS[5][:, 0:128], lhsT=wup[:, k, fb * 128:(fb + 1) * 128], rhs=h2T[:, k, 512:640], start=(k == 0), stop=(k == 7)), [bu, b_h2T], [pb[5]])
                A(lambda e, ub=ub, usb=usb: e.activation(out=usb[:, 1:9, 1:65], in_=PS[ub][:, :].rearrange("p (r c) -> p r c", r=8), func=AF.Copy), [pb[ub]], [b_usb])
                A(lambda e, usb=usb: e.activation(out=usb[:, 0:10:9, 1:65], in_=PS[5][:, 0:128].rearrange("p (r c) -> p r c", r=2), func=AF.Copy), [pb[5]], [b_usb])
                G(lambda e, dg=dg, fb=fb: e.tensor_tensor(out=dg, in0=identb.unsqueeze(1).to_broadcast([128, 9, 128]), in1=cw[:, fb, :].unsqueeze(2).to_broadcast([128, 9, 128]), op=ALU.mult), [b_identb, b_fmp], [b_dg])
                if fb > 0:
                    conv(fb - 1)
                for k in range(8):
                    M(lambda e, k=k, vb=vb, fb=fb: e.matmul(PS[vb][:, :], lhsT=wup[:, k, FH + fb * 128:FH + (fb + 1) * 128], rhs=h2T[:, k, 0:512], start=(k == 0), stop=(k == 7)), [bv, b_h2T], [pb[vb]])
                yield
            conv(NFB - 1)
            yield
            for tt in range(4):
                for hf in range(2):
                    bank = 7 if hf == 0 else 1
                    for fb in range(NFB):
                        M(lambda e, tt=tt, hf=hf, fb=fb, bank=bank: e.matmul(PS[bank][:, :], lhsT=aT[:, fb, tt * 128:(tt + 1) * 128], rhs=wdn[:, fb, hf * 512:(hf + 1) * 512], start=(fb == 0), stop=(fb == NFB - 1)), [b_aT, bwd[fb // 11]], [pb[bank]])
                    A(lambda e, hf=hf, bank=bank: e.activation(out=junk[:, hf * 512:(hf + 1) * 512], in_=PS[bank][:, :], func=AF.Square, accum_out=ssx[:, hf:hf + 1]), [pb[bank]], [b_junk, b_ssx])
                G(lambda e: e.tensor_tensor(out=ssx[:, 2:3], in0=ssx[:, 0:1], in1=ssx[:, 1:2], op=ALU.add), [b_ssx], [b_ssx])
                G(lambda e: e.tensor_scalar(out=ssx[:, 2:3], in0=ssx[:, 2:3], scalar1=1.0 / D, scalar2=EPS, op0=ALU.mult, op1=ALU.add), [b_ssx], [b_ssx])
                G(lambda e: e.tensor_tensor(out=ssx[:, 3:4], in0=ssx[:, 2:3], in1=mhalf[:, 0:1], op=ALU.pow), [b_ssx, b_mhalf], [b_ssx])
                ld("sync", res, x1s[t0 + tt * 128:t0 + (tt + 1) * 128, :], b_res)
                for hf in range(2):
                    bank = 7 if hf == 0 else 1
                    V(lambda e, hf=hf, bank=bank: e.scalar_tensor_tensor(out=tmp, in0=PS[bank][:, :], scalar=ssx[:, 3:4], in1=Gt[:, 1, hf * 512:(hf + 1) * 512], op0=ALU.mult, op1=ALU.mult), [pb[bank], b_ssx, b_Gt], [b_tmp])
                    G(lambda e, hf=hf: e.tensor_tensor(out=res[:, hf * 512:(hf + 1) * 512], in0=res[:, hf * 512:(hf + 1) * 512], in1=tmp, op=ALU.add), [b_res, b_tmp], [b_res])
                last[0] = ld("sync", out[t0 + tt * 128:t0 + (tt + 1) * 128, :], res, b_res, [b_res])
                yield

        interleave(frontgen(0))
        for it in range(nt):
            if it + 1 < nt:
                interleave(body(it), frontgen(it + 1))
            else:
                interleave(body(it))
        return last[0]

    scan_pass(1, ntiles)
    scan_pass(0, ntiles)
    out_pass(ntiles)
    last = ffn_pass(nffn)
    P.barrier()
    P.emit(final_waits=[o for o in [last] if o is not None])
    st.close()
    return nc


def _prep(inputs):
    f = lambda a: np.ascontiguousarray(np.asarray(a, dtype=np.float32))
    x = f(inputs["x"]); c = f(inputs["c"]); ctx = f(inputs["ctx"]); c_ctx = f(inputs["c_ctx"])
    ada_b = f(inputs["ada_b"])[0]
    fm = lambda v: v.reshape(-1, 128).T
    cst = np.zeros((128, 3, 128), np.float32)
    cst[:, 0, :] = np.eye(128, dtype=np.float32)
    j = np.arange(128)[:, None]; i = np.arange(128)[None, :]
    same = (j // 64) == (i // 64)
    cst[:, 1, :] = (same & (j <= i)).astype(np.float32)
    cst[:, 2, :] = (same & (j >= i)).astype(np.float32)
    rowp = np.concatenate([ada_b[2 * D:3 * D], ada_b[5 * D:6 * D], f(inputs["mix_post_g"])[0], f(inputs["ffn_post_g"])[0]])
    rowp = np.ascontiguousarray(np.broadcast_to(rowp[None, :], (128, 4096)))
    lbl = f(inputs["hg_lb_logits"])
    common = {}
    w2 = np.ascontiguousarray(f(inputs["gla_gk_w2"])[0].transpose(1, 0, 2))
    maps = []
    for b in range(8):
        fmp = np.zeros((128, NFM), np.float32)
        fmp[:, FM_C:FM_C + 8] = fm(c[b])
        fmp[:, FM_CC:FM_CC + 8] = fm(c_ctx)
        for jj, g in enumerate([0, 1, 3, 4]):
            fmp[:, FM_ADAB + 8 * jj:FM_ADAB + 8 * jj + 8] = fm(ada_b[g * D:(g + 1) * D])
        fmp[:, FM_PRE1:FM_PRE1 + 8] = fm(f(inputs["mix_pre_g"])[0])
        fmp[:, FM_PRE2:FM_PRE2 + 8] = fm(f(inputs["ffn_pre_g"])[0])
        fmp[:, FM_LB:FM_LB + 16] = lbl.reshape(2, 2, 4, 128).transpose(3, 0, 1, 2).reshape(128, 16)
        fmp[:, FM_GKB:FM_GKB + 4] = f(inputs["gla_gk_b"])[0].reshape(2, 2, 128).transpose(2, 0, 1).reshape(128, 4)
        fmp[:, FM_ON] = f(inputs["hg_onorm_g"])[0]
        fmp[:, FM_ON + 1] = f(inputs["gla_onorm_g"])[0]
        fmp[:, FM_CW:FM_CW + 198] = f(inputs["ffn_conv_w"])[0].reshape(9, NFB, 128).transpose(2, 1, 0).reshape(128, 198)
        fmp[:, FM_CB:FM_CB + NFB] = fm(f(inputs["ffn_conv_b"])[0])
        maps.append({
            "x": x[b], "ctx": ctx[b], "fmp": fmp, "rowp": rowp, "cst": cst, "w2": w2,
            "ada_w": f(inputs["ada_w"])[0], "w_in": f(inputs["w_in"])[0], "w_a": f(inputs["w_branch_a"])[0],
            "w_b": f(inputs["w_branch_b"])[0], "w_o": f(inputs["w_out"])[0], "w_up": f(inputs["ffn_w_up"])[0],
            "w_dn": f(inputs["ffn_w_down"])[0],
        })
    return maps


def kernel(**inputs):
    maps = _prep(inputs)
    nc = build_nc()
    res = run_bass_kernel_spmd(nc, maps, core_ids=list(range(8)))
    return np.stack([np.asarray(r["out"], dtype=np.float32) for r in res.results], axis=0)
```
